# Optimizing a Trainium2 kernel written in Bass

```python
import math
import jax, jax.numpy as jnp
from jax import lax
import numpy as np

D_MODEL = 1024
BATCH = 4
SEQ = 8192
DEPTH = 4
DEC_BATCH = 8
DEC_SEQ = 16
PAST_LEN = 2048

CHUNK = 64
N_META = 16
N_MIXERS = 2
N_A = (DEPTH + 1) // 2
N_B = DEPTH // 2
N_HEADS = 8
HEAD_DIM = D_MODEL // N_HEADS
HD = N_HEADS * HEAD_DIM
IDX_HEADS = 8
IDX_DIM = 64
TOPK_MAX = 256
N_BUCKETS = 32
MAX_DISTANCE = 128
D_FF = 2816
CONV_W = 3
Q_BLOCK = 128
ALPHA = (2 * DEPTH) ** 0.25
BETA = (8 * DEPTH) ** -0.25
LN_EPS = 1e-5
IDX_Q_OFF = 3 * HD
IDX_K_OFF = IDX_Q_OFF + IDX_HEADS * IDX_DIM
IDX_W_OFF = IDX_K_OFF + IDX_DIM
A_IN = IDX_W_OFF + IDX_HEADS
B_IN = 3 * HD

kernel_name = 'dsa_stickbreak_convffn_stream'


def layer_norm(x, g, b):
    xf = x.astype(jnp.float32)
    mu = jnp.mean(xf, -1, keepdims=True)
    var = jnp.mean(jnp.square(xf - mu), -1, keepdims=True)
    y = (xf - mu) * lax.rsqrt(var + LN_EPS) * g.astype(jnp.float32) + b.astype(jnp.float32)
    return y.astype(x.dtype)


def rel_bucket(rel):
    nb = N_BUCKETS // 2
    max_exact = nb // 2
    ret = jnp.where(rel > 0, nb, 0)
    n = jnp.abs(rel)
    nf = jnp.maximum(n, 1).astype(jnp.float32)
    large = max_exact + (jnp.log(nf / max_exact) / math.log(MAX_DISTANCE / max_exact)
                         * (nb - max_exact)).astype(jnp.int32)
    large = jnp.minimum(large, nb - 1)
    return ret + jnp.where(n < max_exact, n, large)


def chunk_id(pos, n_lead):
    return jnp.where(pos < n_lead, -1, (pos - n_lead) // CHUNK)


def map_query_blocks(fn, batched, vectors):
    b, t = batched[0].shape[:2]
    n_blk = -(-t // Q_BLOCK)
    pad = n_blk * Q_BLOCK - t
    def split_b(a):
        a = jnp.pad(a, [(0, 0), (0, pad)] + [(0, 0)] * (a.ndim - 2))
        return jnp.swapaxes(a.reshape((b, n_blk, Q_BLOCK) + a.shape[2:]), 0, 1)
    def split_v(a):
        return jnp.pad(a, (0, pad), mode='edge').reshape(n_blk, Q_BLOCK)
    blocks = tuple(split_b(a) for a in batched) + tuple(split_v(a) for a in vectors)
    out = lax.map(lambda blk: fn(*blk), blocks)
    out = jnp.swapaxes(out, 0, 1)
    return out.reshape((b, n_blk * Q_BLOCK) + out.shape[3:])[:, :t]


def dsa_attend(q, qi, wi, k, v, ki, q_pos, k_pos, q_chunk, k_chunk, rel_bias, topk):
    s = jnp.einsum('bqhd,bld->bqhl', qi, ki).astype(jnp.float32)
    score = jnp.einsum('bqhl,bqh->bql', jax.nn.relu(s), wi.astype(jnp.float32))
    visible = k_chunk[None, :] <= q_chunk[:, None]
    score = jnp.where(visible[None], score, -jnp.inf)
    _, idx = lax.top_k(score, topk)
    gather = jax.vmap(lambda a, i: a[i])
    k_sel = gather(k, idx)
    v_sel = gather(v, idx)
    logits = jnp.einsum('bqhd,bqkhd->bhqk', q, k_sel).astype(jnp.float32) * (HEAD_DIM ** -0.5)
    rel = k_pos[idx] - q_pos[None, :, None]
    bias = rel_bias.astype(jnp.float32)[rel_bucket(rel)]
    logits = logits + jnp.transpose(bias, (0, 3, 1, 2))
    valid = k_chunk[idx] <= q_chunk[None, :, None]
    logits = jnp.where(valid[:, None], logits, -jnp.inf)
    p = jax.nn.softmax(logits, axis=-1).astype(v.dtype)
    return jnp.einsum('bhqk,bqkhd->bqhd', p, v_sel)


def stick_breaking(q, k, v, q_pos, k_pos):
    z = jnp.einsum('bqhd,blhd->bhql', q, k).astype(jnp.float32) * (HEAD_DIM ** -0.5)
    causal = (k_pos[None, :] < q_pos[:, None])[None, None]
    log_1m = jnp.where(causal, jax.nn.log_sigmoid(-z), 0.0)
    between = lax.cumsum(log_1m, axis=3, reverse=True) - log_1m
    a = jnp.where(causal, jnp.exp(jax.nn.log_sigmoid(z) + between), 0.0)
    return jnp.einsum('bhql,blhd->bqhd', a.astype(v.dtype), v)


def project_a(x, w_in):
    b, t, _ = x.shape
    h = x @ w_in
    q = h[..., :HD].reshape(b, t, N_HEADS, HEAD_DIM)
    k = h[..., HD:2 * HD].reshape(b, t, N_HEADS, HEAD_DIM)
    v = h[..., 2 * HD:3 * HD].reshape(b, t, N_HEADS, HEAD_DIM)
    qi = h[..., IDX_Q_OFF:IDX_K_OFF].reshape(b, t, IDX_HEADS, IDX_DIM)
    ki = h[..., IDX_K_OFF:IDX_W_OFF]
    wi = h[..., IDX_W_OFF:] * ((IDX_HEADS * IDX_DIM) ** -0.5)
    return q, k, v, qi, ki, wi


def project_b(x, w_in):
    b, t, _ = x.shape
    h = (x @ w_in).reshape(b, t, 3, N_HEADS, HEAD_DIM)
    return h[:, :, 0], h[:, :, 1], h[:, :, 2]


def mixer_a_prompt(x, w_in, w_out, rel_bias, topk):
    b, t, _ = x.shape
    q, k, v, qi, ki, wi = project_a(x, w_in)
    pos = jnp.arange(t, dtype=jnp.int32)
    chunk = chunk_id(pos, N_META)
    fn = lambda qb, qib, wib, pb, cb: dsa_attend(qb, qib, wib, k, v, ki, pb, pos, cb, chunk, rel_bias, topk)
    o = map_query_blocks(fn, (q, qi, wi), (pos, chunk))
    return o.reshape(b, t, HD) @ w_out, k, v, ki


def mixer_a_sample(x, ck, cv, cki, w_in, w_out, rel_bias, topk):
    b, t, _ = x.shape
    p = ck.shape[1]
    q, k, v, qi, ki, wi = project_a(x, w_in)
    k_all = jnp.concatenate([ck.astype(k.dtype), k], axis=1)
    v_all = jnp.concatenate([cv.astype(v.dtype), v], axis=1)
    ki_all = jnp.concatenate([cki.astype(ki.dtype), ki], axis=1)
    k_pos = jnp.arange(p + t, dtype=jnp.int32)
    q_pos = p + jnp.arange(t, dtype=jnp.int32)
    o = dsa_attend(q, qi, wi, k_all, v_all, ki_all, q_pos, k_pos,
                   chunk_id(q_pos, 0), chunk_id(k_pos, 0), rel_bias, topk)
    return o.reshape(b, t, HD) @ w_out, k, v, ki


def mixer_b_prompt(x, w_in, w_out):
    b, t, _ = x.shape
    q, k, v = project_b(x, w_in)
    pos = jnp.arange(t, dtype=jnp.int32)
    fn = lambda qb, pb: stick_breaking(qb, k, v, pb, pos)
    o = map_query_blocks(fn, (q,), (pos,))
    return o.reshape(b, t, HD) @ w_out, k, v


def mixer_b_sample(x, ck, cv, w_in, w_out):
    b, t, _ = x.shape
    p = ck.shape[1]
    q, k, v = project_b(x, w_in)
    k_all = jnp.concatenate([ck.astype(k.dtype), k], axis=1)
    v_all = jnp.concatenate([cv.astype(v.dtype), v], axis=1)
    k_pos = jnp.arange(p + t, dtype=jnp.int32)
    q_pos = p + jnp.arange(t, dtype=jnp.int32)
    o = stick_breaking(q, k_all, v_all, q_pos, k_pos)
    return o.reshape(b, t, HD) @ w_out, k, v


def conv_ffn(x, left, w_gate, w_up, conv_w, conv_b, w_down):
    t = x.shape[1]
    g = x @ w_gate
    u = x @ w_up
    gp = jnp.concatenate([left.astype(g.dtype), g], axis=1)
    gc = conv_b + conv_w[0] * gp[:, 0:t]
    for i in range(1, CONV_W):
        gc = gc + conv_w[i] * gp[:, i:i + t]
    h = jax.nn.gelu(gc) * u
    return h @ w_down, gp[:, gp.shape[1] - (CONV_W - 1):]


def setup_inputs(seed: int = 0) -> dict:
    key = jax.random.key(seed)
    ks = jax.random.split(key, 24)
    def nrm(k, shape, scale=1.0):
        return jax.random.normal(k, shape, jnp.float32) * scale
    return {
        'x_prompt': nrm(ks[0], (BATCH, SEQ, D_MODEL)),
        'x_sample': nrm(ks[1], (DEC_BATCH, DEC_SEQ, D_MODEL)),
        'cache_a_k': nrm(ks[2], (N_A, DEC_BATCH, PAST_LEN, N_HEADS, HEAD_DIM)),
        'cache_a_v': nrm(ks[3], (N_A, DEC_BATCH, PAST_LEN, N_HEADS, HEAD_DIM)),
        'cache_a_idx_k': nrm(ks[4], (N_A, DEC_BATCH, PAST_LEN, IDX_DIM)),
        'cache_b_k': nrm(ks[5], (N_B, DEC_BATCH, PAST_LEN, N_HEADS, HEAD_DIM)),
        'cache_b_v': nrm(ks[6], (N_B, DEC_BATCH, PAST_LEN, N_HEADS, HEAD_DIM)),
        'state_ffn_conv': nrm(ks[7], (DEPTH, DEC_BATCH, CONV_W - 1, D_FF)),
        'meta_tokens': nrm(ks[8], (N_META, D_MODEL)),
        'rel_bias': nrm(ks[9], (N_BUCKETS, N_HEADS), 0.5),
        'w_a_in': nrm(ks[10], (N_A, D_MODEL, A_IN), D_MODEL ** -0.5),
        'w_a_out': nrm(ks[11], (N_A, HD, D_MODEL), BETA * HD ** -0.5),
        'w_b_in': nrm(ks[12], (N_B, D_MODEL, B_IN), D_MODEL ** -0.5),
        'w_b_out': nrm(ks[13], (N_B, HD, D_MODEL), BETA * HD ** -0.5),
        'ln1_g': 1.0 + nrm(ks[14], (DEPTH, D_MODEL), 0.05),
        'ln1_b': nrm(ks[15], (DEPTH, D_MODEL), 0.02),
        'ln2_g': 1.0 + nrm(ks[16], (DEPTH, D_MODEL), 0.05),
        'ln2_b': nrm(ks[17], (DEPTH, D_MODEL), 0.02),
        'w_ffn_gate': nrm(ks[18], (DEPTH, D_MODEL, D_FF), D_MODEL ** -0.5),
        'w_ffn_up': nrm(ks[19], (DEPTH, D_MODEL, D_FF), D_MODEL ** -0.5),
        'ffn_conv_w': nrm(ks[20], (DEPTH, CONV_W, D_FF), CONV_W ** -0.5),
        'ffn_conv_b': nrm(ks[21], (DEPTH, D_FF), 0.02),
        'w_ffn_down': nrm(ks[22], (DEPTH, D_FF, D_MODEL), BETA * D_FF ** -0.5),
    }


def reference(x_prompt, x_sample, cache_a_k, cache_a_v, cache_a_idx_k, cache_b_k, cache_b_v,
              state_ffn_conv, meta_tokens, rel_bias, w_a_in, w_a_out, w_b_in, w_b_out,
              ln1_g, ln1_b, ln2_g, ln2_b, w_ffn_gate, w_ffn_up, ffn_conv_w, ffn_conv_b, w_ffn_down):
    b = x_prompt.shape[0]
    meta = jnp.broadcast_to(meta_tokens.astype(x_prompt.dtype)[None], (b, N_META, D_MODEL))
    xp = jnp.concatenate([meta, x_prompt], axis=1)
    xs = x_sample
    topk_p = min(TOPK_MAX, x_prompt.shape[1] // 4)
    topk_s = min(TOPK_MAX, (cache_a_k.shape[2] + x_sample.shape[1]) // 4)
    zero_left = jnp.zeros((b, CONV_W - 1, D_FF), xp.dtype)
    a_k_p, a_v_p, a_ik_p, a_k_s, a_v_s, a_ik_s = [], [], [], [], [], []
    b_k_p, b_v_p, b_k_s, b_v_s = [], [], [], []
    conv_p, conv_s = [], []
    for i in range(DEPTH):
        j = i // N_MIXERS
        if i % N_MIXERS == 0:
            mp, kp, vp, kip = mixer_a_prompt(xp, w_a_in[j], w_a_out[j], rel_bias, topk_p)
            ms, kn, vn, kin = mixer_a_sample(xs, cache_a_k[j], cache_a_v[j], cache_a_idx_k[j],
                                             w_a_in[j], w_a_out[j], rel_bias, topk_s)
            a_k_p.append(kp); a_v_p.append(vp); a_ik_p.append(kip)
            a_k_s.append(kn); a_v_s.append(vn); a_ik_s.append(kin)
        else:
            mp, kp, vp = mixer_b_prompt(xp, w_b_in[j], w_b_out[j])
            ms, kn, vn = mixer_b_sample(xs, cache_b_k[j], cache_b_v[j], w_b_in[j], w_b_out[j])
            b_k_p.append(kp); b_v_p.append(vp)
            b_k_s.append(kn); b_v_s.append(vn)
        xp = layer_norm(ALPHA * xp + mp, ln1_g[i], ln1_b[i])
        xs = layer_norm(ALPHA * xs + ms, ln1_g[i], ln1_b[i])
        fp, cp = conv_ffn(xp, zero_left, w_ffn_gate[i], w_ffn_up[i], ffn_conv_w[i], ffn_conv_b[i], w_ffn_down[i])
        fs, cs = conv_ffn(xs, state_ffn_conv[i], w_ffn_gate[i], w_ffn_up[i], ffn_conv_w[i], ffn_conv_b[i], w_ffn_down[i])
        conv_p.append(cp); conv_s.append(cs)
        xp = layer_norm(ALPHA * xp + fp, ln2_g[i], ln2_b[i])
        xs = layer_norm(ALPHA * xs + fs, ln2_g[i], ln2_b[i])
    y_prompt = xp[:, N_META:]
    return (y_prompt, xs,
            jnp.stack(a_k_p), jnp.stack(a_v_p), jnp.stack(a_ik_p),
            jnp.stack(b_k_p), jnp.stack(b_v_p), jnp.stack(conv_p),
            jnp.stack(a_k_s), jnp.stack(a_v_s), jnp.stack(a_ik_s),
            jnp.stack(b_k_s), jnp.stack(b_v_s), jnp.stack(conv_s))
```

```python
import contextlib
import math
import numpy as np
import concourse.bass as bass
import concourse.mybir as mybir
from concourse.bass_utils import run_bass_kernel_spmd

F32 = mybir.dt.float32
BF16 = mybir.dt.bfloat16
ALU = mybir.AluOpType
AF = mybir.ActivationFunctionType
AX = mybir.AxisListType

ENGS = ('pe', 'act', 'dve', 'pool', 'sp')
SAME_ENG_SYNC = True


class Op:
    __slots__ = ('eng', 'fn', 'deps', 'is_dma', 'chan', 'cum', 'signal', 'count', 'idx')


class Glob:
    def __init__(self, nc, stack):
        self.nc = nc
        self.stack = stack
        self.eng_sem = {e: stack.enter_context(nc.semaphore('s_' + e)) for e in ENGS}
        self.eng_cnt = {e: 0 for e in ENGS}
        self.chan_sem = {}
        self.chan_cum = {}

    def chan(self, name):
        if name not in self.chan_sem:
            self.chan_sem[name] = self.stack.enter_context(self.nc.semaphore('c_' + name))
            self.chan_cum[name] = 0
        return self.chan_sem[name]


class Sched:
    def __init__(self, glob):
        self.glob = glob
        self.ops = {e: [] for e in ENGS}
        self.state = {}
        self.chans = set()
        self.n = 0

    @staticmethod
    def _key(r):
        if isinstance(r, tuple):
            return (id(r[0]),) + tuple(r[1:])
        return (id(r),)

    def add(self, eng, fn, reads=(), writes=(), dma=False, chan=None):
        op = Op()
        op.eng = eng
        op.fn = fn
        op.is_dma = dma
        op.signal = False
        op.count = None
        op.chan = chan
        op.idx = self.n
        self.n += 1
        deps = {}
        for r in reads:
            st = self.state.get(self._key(r))
            if st is not None and st[0] is not None:
                deps[st[0].idx] = st[0]
        for w in writes:
            st = self.state.get(self._key(w))
            if st is not None:
                w0 = st[0]
                if w0 is not None:
                    if not (dma and w0.is_dma and w0.chan == chan and not st[1] and not st[2]):
                        deps[w0.idx] = w0
                for o in st[1].values():
                    deps[o.idx] = o
                for o in st[2]:
                    deps[o.idx] = o
        op.deps = list(deps.values())
        for r in reads:
            st = self.state.setdefault(self._key(r), [None, {}, []])
            if dma:
                st[2].append(op)
            else:
                st[1][eng] = op
        for w in writes:
            self.state[self._key(w)] = [op, {}, []]
        if dma:
            assert chan is not None
            self.glob.chan(chan)
            self.glob.chan_cum[chan] += 1
            op.cum = self.glob.chan_cum[chan]
            self.chans.add(chan)
        self.ops[eng].append(op)
        return op

    def emit(self, nc, stack, tag):
        G = self.glob
        for e in ENGS:
            for op in self.ops[e]:
                for d in op.deps:
                    if d.is_dma:
                        continue
                    if d.eng == e and (e == 'pe' or not SAME_ENG_SYNC):
                        continue
                    d.signal = True
        eng_sem = G.eng_sem
        for e in ENGS:
            c = G.eng_cnt[e]
            for op in self.ops[e]:
                if op.signal and not op.is_dma:
                    c += 1
                    op.count = c
            G.eng_cnt[e] = c
        chan_sem = G.chan_sem
        final = [(chan_sem[ch], 16 * G.chan_cum[ch]) for ch in sorted(self.chans)]
        sched = self

        def run(e, h):
            known = {}
            for op in sched.ops[e]:
                waits = {}
                for d in op.deps:
                    if d.is_dma:
                        s, v = chan_sem[d.chan], 16 * d.cum
                    else:
                        if d.eng == e and (e == 'pe' or not SAME_ENG_SYNC):
                            continue
                        s, v = eng_sem[d.eng], d.count
                    k = id(s)
                    if k not in waits or waits[k][1] < v:
                        waits[k] = (s, v)
                for k, (s, v) in waits.items():
                    if known.get(k, 0) >= v:
                        continue
                    h.wait_ge(s, v)
                    STATS[tag + ':' + e + ':wait'] = STATS.get(tag + ':' + e + ':wait', 0) + 1
                    known[k] = v
                STATS[tag + ':' + e] = STATS.get(tag + ':' + e, 0) + 1
                ins = op.fn(h)
                if op.is_dma:
                    ins.then_inc(chan_sem[op.chan], 16)
                elif op.signal:
                    ins.then_inc(eng_sem[e], 1)
            if e == 'sp':
                for s, v in final:
                    h.wait_ge(s, v)

        with nc.Block() as block:
            @block.sync
            def _(h):
                run('sp', h)

            @block.tensor
            def _(h):
                run('pe', h)

            @block.scalar
            def _(h):
                run('act', h)

            @block.vector
            def _(h):
                run('dve', h)

            @block.gpsimd
            def _(h):
                run('pool', h)


_uid = [0]
STATS = {}


class Phase:
    glob = None

    def __init__(self, nc, name):
        self.nc = nc
        self.name = name
        self.st = contextlib.ExitStack()
        self.S = Sched(Phase.glob)
        self.rr = 0

    def sb(self, shape, dt, name='t'):
        _uid[0] += 1
        return self.st.enter_context(self.nc.sbuf_tensor('%s_%s%d' % (self.name, name, _uid[0]), list(shape), dt))

    def ps(self, shape, dt=F32, name='p'):
        _uid[0] += 1
        return self.st.enter_context(self.nc.psum_tensor('%s_%s%d' % (self.name, name, _uid[0]), list(shape), dt))

    def finish(self):
        _uid[0] += 1
        self.S.emit(self.nc, self.st, self.name)
        self.st.close()

    def op(self, eng, fn, r=(), w=()):
        return self.S.add(eng, fn, r, w)

    def dma(self, eng, out, in_, r, w, chan, slow=False):
        if eng == 'pool':
            eng = 'sp'
        if slow:
            fn = lambda e: e.dma_start(out=out, in_=in_, allow_slow_non_contiguous=True)
        else:
            fn = lambda e: e.dma_start(out=out, in_=in_)
        return self.S.add(eng, fn, r, w, dma=True, chan=chan)

    def store_rows(self, cfg, s, dst, q0, n, src, r, chan, yrow=False, key=None):
        off = 16 if yrow else 0
        if not (cfg.SPLIT and s.kind == 'p'):
            return self.dma('pool', dst(q0 - off), src, r, [], chan)
        owner = 1 if q0 >= cfg.PH else 0
        row0 = q0 - off - owner * (cfg.PH - off)
        out = dst(row0)
        mk = self.mk
        res = ('dramout', key)
        if owner == 0:
            self.ts('pool', src, src, mk[0:n, 0:1], None, ALU.mult, None, list(r) + [mk], list(r))
            return self.dma('pool', out, src, r, [res], chan)
        self.rmwi = getattr(self, 'rmwi', 0) + 1
        tmp = self.rmw[self.rmwi % len(self.rmw)]
        w = src.shape[-1]
        self.dma('sp', tmp[0:n, 0:w], out, [res], [tmp], 'rmw%d' % (self.rmwi % len(self.rmw)))
        self.stt(src, src, mk[0:n, 1:2], tmp[0:n, 0:w], ALU.mult, ALU.add, list(r) + [mk, tmp], list(r))
        return self.dma('pool', out, src, r, [], chan)

    def init_split(self, cfg, I, nslots):
        if cfg.SPLIT:
            self.mk = self.sb([128, 2], F32, 'mk')
            self.dma('sp', self.mk[:], I['maskf'].partition_broadcast(128), [], [self.mk], 'mk')
            self.rmw = [self.sb([128, 1024], F32, 'rmw') for _ in range(nslots)]

    def mm(self, out, lhsT, rhs, start, stop, r, w):
        return self.S.add('pe', lambda e: e.matmul(out, lhsT=lhsT, rhs=rhs, start=start, stop=stop), r, w)

    def tr(self, out, in_, ident, r, w):
        return self.S.add('pe', lambda e: e.transpose(out=out, in_=in_, identity=ident), r, w)

    def act(self, out, in_, func, r, w, bias=None, scale=None, accum=None):
        kw = {}
        if bias is not None:
            kw['bias'] = bias
        if scale is not None:
            kw['scale'] = scale
        if accum is not None:
            kw['accum_out'] = accum
        return self.S.add('act', lambda e: e.activation(out=out, in_=in_, func=func, **kw), r, w)

    def copy(self, eng, out, in_, r, w):
        if eng == 'act':
            return self.act(out, in_, AF.Copy, r, w)
        return self.S.add(eng, lambda e: e.tensor_copy(out=out, in_=in_), r, w)

    def evac(self, out, in_, r, w):
        self.rr += 1
        return self.copy('act' if self.rr % 2 else 'dve', out, in_, r, w)

    def ts(self, eng, out, in0, s1, s2, op0, op1, r, w, accum=None):
        if op1 is None:
            fn = lambda e: e.tensor_scalar(out=out, in0=in0, scalar1=s1, scalar2=None, op0=op0)
        elif accum is None:
            fn = lambda e: e.tensor_scalar(out=out, in0=in0, scalar1=s1, scalar2=s2, op0=op0, op1=op1)
        else:
            fn = lambda e: e.tensor_scalar(out=out, in0=in0, scalar1=s1, scalar2=s2, op0=op0, op1=op1, accum_out=accum)
        return self.S.add(eng, fn, r, w)

    def stt(self, out, in0, scalar, in1, op0, op1, r, w):
        return self.S.add('dve', lambda e: e.scalar_tensor_tensor(out=out, in0=in0, scalar=scalar, in1=in1, op0=op0, op1=op1), r, w)

    def tt(self, eng, out, in0, in1, op, r, w):
        return self.S.add(eng, lambda e: e.tensor_tensor(out=out, in0=in0, in1=in1, op=op), r, w)

    def memset(self, eng, ap, val, w):
        return self.S.add(eng, lambda e: e.memset(ap, val), (), w)


class Cfg:
    def __init__(self, S=8192, PAST=2048, TS=16, DEPTH=4):
        self.D = 1024
        self.H = 8
        self.HD = 128
        self.DFF = 2816
        self.NF = 22
        self.IH = 8
        self.ID = 64
        self.NM = 16
        self.S = S
        self.PAST = PAST
        self.TS = TS
        self.DEPTH = DEPTH
        self.NA = (DEPTH + 1) // 2
        self.NB = DEPTH // 2
        self.P = self.NM + S
        self.PS = PAST + TS
        self.AIN = 3656
        self.BIN = 3072
        self.TOPK_P = min(256, S // 4)
        self.TOPK_S = min(256, self.PS // 4)
        self.ALPHA = (2 * DEPTH) ** 0.25
        self.NIT = 20
        self.BWIN = 8
        self.upto = None
        self.NSAMP = 1
        self.ALIAS = False
        self.SPLIT = False
        self.PH = 16 + 128 * ((S // 128) // 2)


BIG = 1.0e4
RR = 383


def rel_bucket_np(rel):
    import jax
    import jax.numpy as jnp

    def f(rel):
        nb = 16
        max_exact = 8
        ret = jnp.where(rel > 0, nb, 0)
        n = jnp.abs(rel)
        nf = jnp.maximum(n, 1).astype(jnp.float32)
        large = max_exact + (jnp.log(nf / max_exact) / math.log(128 / max_exact) * (nb - max_exact)).astype(jnp.int32)
        large = jnp.minimum(large, nb - 1)
        return ret + jnp.where(n < max_exact, n, large)

    cpu = jax.devices('cpu')[0]
    with jax.default_device(cpu):
        out = jax.jit(f, backend='cpu')(jax.device_put(np.asarray(rel, dtype=np.int32), cpu))
        return np.asarray(out)


class Stream:
    pass


def make_streams(cfg):
    sp = Stream()
    sp.name = 'p'
    sp.NQ = cfg.P
    sp.NK = cfg.P
    sp.koff = 0
    sp.qtiles = [(0, 16)] + [(16 + 128 * i, 128) for i in range(cfg.S // 128)]
    sp.kblocks = list(sp.qtiles)
    sp.own = list(range(len(sp.qtiles)))
    ng = (len(sp.qtiles) - 1) // 4
    sp.groups = [[0]] + [[1 + 4 * g + i for i in range(4)] for g in range(ng)]
    sp.groups2 = [[0]] + [[1 + 2 * g + i for i in range(2)] for g in range(2 * ng)]
    sp.topk = cfg.TOPK_P
    sp.chunked = True
    sp.kind = 'p'
    sp.sidx = 0
    sss = []
    for i in range(cfg.NSAMP):
        ss = Stream()
        ss.name = 's%d' % i
        ss.kind = 's'
        ss.sidx = i
        ss.NQ = cfg.TS
        ss.NK = cfg.PS
        ss.koff = cfg.PAST
        ss.qtiles = [(0, 16)]
        ss.kblocks = [(128 * b, 128) for b in range(cfg.PAST // 128)] + [(cfg.PAST, 16)]
        ss.own = [len(ss.kblocks) - 1]
        ss.groups = [[0]]
        ss.groups2 = [[0]]
        ss.topk = cfg.TOPK_S
        ss.chunked = False
        sss.append(ss)
    return [sp] + sss


def build(cfg):
    nc = bass.Bass("TRN2", target_bir_lowering=False)
    D, H, DFF, NF = cfg.D, cfg.H, cfg.DFF, cfg.NF
    NA, NB, DEPTH = cfg.NA, cfg.NB, cfg.DEPTH
    streams = make_streams(cfg)
    sp = streams[0]
    NS = cfg.NSAMP

    def din(name, shape, dt=F32):
        return nc.dram_tensor(name, list(shape), dt, kind="ExternalInput").ap()

    def dout(name, shape, dt=F32):
        return nc.dram_tensor(name, list(shape), dt, kind="ExternalOutput").ap()

    def dscr(name, shape, dt):
        if getattr(cfg, 'debug', False):
            return nc.dram_tensor(name, list(shape), dt, kind="ExternalOutput").ap()
        return nc.dram_tensor(name, list(shape), dt).ap()

    I = {}
    I['xp'] = din('xp', [cfg.S // 2 + 1, 2 * D])[0:cfg.S // 2, :].rearrange("a (two d) -> (a two) d", two=2)
    I['xs'] = din('xs', [NS, cfg.TS, D])
    I['meta'] = din('meta', [16, D])
    I['cak'] = din('cak', [NS, NA, cfg.PAST, D])
    I['cav'] = din('cav', [NS, NA, cfg.PAST, D])
    I['caik'] = din('caik', [NS, NA, cfg.PAST, 64])
    I['cbk'] = din('cbk', [NS, max(NB, 1), cfg.PAST, D])
    I['cbv'] = din('cbv', [NS, max(NB, 1), cfg.PAST, D])
    I['sconv'] = din('sconv', [NS, DEPTH, 2, DFF])
    I['relb'] = din('relb', [32, 8])
    I['wain'] = din('wain', [NA, D, cfg.AIN])
    I['waout'] = din('waout', [NA, D, D])
    I['wbin'] = din('wbin', [max(NB, 1), D, cfg.BIN])
    I['wbout'] = din('wbout', [max(NB, 1), D, D])
    for nm in ('ln1g', 'ln1b', 'ln2g', 'ln2b'):
        I[nm] = din(nm, [DEPTH, D])
    I['wg'] = din('wg', [DEPTH, D, DFF])
    I['wu'] = din('wu', [DEPTH, D, DFF])
    I['cw'] = din('cw', [DEPTH, 3, DFF])
    I['cb'] = din('cb', [DEPTH, DFF])
    I['wd'] = din('wd', [DEPTH, DFF, D])
    I['ohr'] = din('ohr', [32, RR])
    if cfg.SPLIT:
        I['maskf'] = din('maskf', [1, 2])
    PR = max(cfg.PH, cfg.P - cfg.PH) if cfg.SPLIT else cfg.P
    YR = max(cfg.PH - 16, cfg.P - cfg.PH) if cfg.SPLIT else cfg.S

    O = {}
    O['yp'] = dout('yp', [YR, D])
    O['ys'] = dout('ys', [NS, cfg.TS, D])
    O['akp'] = dout('akp', [NA, PR, D])
    O['avp'] = dout('avp', [NA, PR, D])
    O['aikp'] = dout('aikp', [NA, PR, 64])
    O['bkp'] = dout('bkp', [max(NB, 1), PR, D])
    O['bvp'] = dout('bvp', [max(NB, 1), PR, D])
    O['convp'] = dout('convp', [DEPTH, 2, DFF])
    O['aks'] = dout('aks', [NS, NA, cfg.TS, D])
    O['avs'] = dout('avs', [NS, NA, cfg.TS, D])
    O['aiks'] = dout('aiks', [NS, NA, cfg.TS, 64])
    O['bks'] = dout('bks', [NS, max(NB, 1), cfg.TS, D])
    O['bvs'] = dout('bvs', [NS, max(NB, 1), cfg.TS, D])
    O['convs'] = dout('convs', [NS, DEPTH, 2, DFF])

    for s in streams:
        n = s.name
        alias = cfg.ALIAS and s.kind == 'p'
        s.X = dscr('X' + n, [16 if alias else s.NQ, D], F32)
        s.QT = dscr('QT' + n, [H, 128, s.NQ], BF16)
        s.KT = dscr('KT' + n, [H, 128, s.NK], BF16)
        s.V = dscr('V' + n, [H, 128, len(s.kblocks), 128], BF16)
        s.QIT = dscr('QIT' + n, [8 * 64, s.NQ], BF16)
        s.KIT = dscr('KIT' + n, [64, s.NK], BF16)
        s.WI = dscr('WI' + n, [s.NQ, 8], F32)
        s.OA = dscr('OA' + n, [16 if alias else s.NQ, D], BF16)
        if alias:
            oay = O['yp'].bitcast(BF16).rearrange("a (two d) -> (a two) d", two=2)
            assert tuple(oay.shape) == (cfg.S, D), oay.shape
            s.xrows = (lambda q0, n_, s=s: s.X[0:n_, :] if q0 == 0 else I['xp'][q0 - 16:q0 - 16 + n_, :])
            s.oarows = (lambda q0, n_, s=s, oay=oay: s.OA[0:n_, :] if q0 == 0 else oay[q0 - 16:q0 - 16 + n_, :])
        else:
            s.xrows = (lambda q0, n_, s=s: s.X[q0:q0 + n_, :])
            s.oarows = (lambda q0, n_, s=s: s.OA[q0:q0 + n_, :])
    VB = nc.dram_tensor('VB', [8, RR], F32)
    FO = dict(O)
    for s in streams:
        if s.kind == 'p':
            s.ko = {True: FO['akp'], False: FO['bkp']}
            s.vo = {True: FO['avp'], False: FO['bvp']}
            s.io = FO['aikp']
            s.convo = O['convp']
        else:
            i_ = s.sidx
            s.ko = {True: O['aks'][i_], False: O['bks'][i_]}
            s.vo = {True: O['avs'][i_], False: O['bvs'][i_]}
            s.io = O['aiks'][i_]
            s.convo = O['convs'][i_]
            s.ck = {True: I['cak'][i_], False: I['cbk'][i_]}
            s.cv = {True: I['cav'][i_], False: I['cbv'][i_]}
            s.cik = I['caik'][i_]
            s.sconv = I['sconv'][i_]
            s.yo = O['ys'][i_]

    def x0_src(s, q0, n):
        if s.kind == 's':
            return I['xs'][s.sidx, q0:q0 + n, :]
        if q0 == 0:
            return I['meta'][0:n, :]
        return I['xp'][q0 - 16:q0 - 16 + n, :]

    top = contextlib.ExitStack()
    nph = [0]

    def phase_done():
        nph[0] += 1
        return cfg.upto is not None and nph[0] >= cfg.upto

    with top:
        Phase.glob = Glob(nc, top)
        def ctile(shape, dt, name):
            return top.enter_context(nc.sbuf_tensor('c_' + name, list(shape), dt))

        ident_f = ctile([128, 128], F32, 'identf')
        ident_b = ctile([128, 128], BF16, 'identb')
        i30k = ctile([128, 128], BF16, 'i30k')
        negtri = ctile([128, 128], BF16, 'negtri')
        negones = ctile([128, 128], BF16, 'negones')
        vm = ctile([128, 128], F32, 'vm')
        vmpos = ctile([128, 128], F32, 'vmpos')
        pow2 = ctile([128, cfg.NIT + 1], F32, 'pow2')
        tz0 = ctile([128, 8, 128], BF16, 'tz0')
        tz1 = ctile([128, 8, 128], BF16, 'tz1')
        tzm = ctile([128, 8, 128], BF16, 'tzm')
        epsc = ctile([128, 1], F32, 'eps')

        ph = Phase(nc, 'init')
        jf = ph.sb([128, 128], F32, 'jf')
        ph.memset('pool', ident_f[:], 0.0, [ident_f])
        ph.op('pool', lambda e: e.affine_select(out=ident_f[:], in_=ident_f[:], pattern=[[-1, 128]], compare_op=ALU.not_equal, fill=1.0, base=0, channel_multiplier=1), [ident_f], [ident_f])
        ph.copy('dve', ident_b[:], ident_f[:], [ident_f], [ident_b])
        ph.ts('dve', i30k[:], ident_f[:], 30000.0, None, ALU.mult, None, [ident_f], [i30k])
        ph.memset('pool', jf[:], 0.0, [jf])
        ph.op('pool', lambda e: e.affine_select(out=jf[:], in_=jf[:], pattern=[[1, 128]], compare_op=ALU.not_equal, fill=1.0, base=-127, channel_multiplier=1), [jf], [jf])
        ph.memset('pool', negtri[:], -1.0, [negtri])
        ph.op('pool', lambda e: e.affine_select(out=negtri[:], in_=negtri[:], pattern=[[-1, 128]], compare_op=ALU.is_ge, fill=0.0, base=0, channel_multiplier=1), [negtri], [negtri])
        ph.memset('pool', negones[:], -1.0, [negones])
        ph.memset('dve', vm[:], 0.0, [vm])
        ph.memset('dve', vm[0:64, 64:128], -BIG, [vm])
        ph.memset('dve', vmpos[:], 0.0, [vmpos])
        ph.memset('dve', vmpos[0:64, 64:128], 2 * BIG, [vmpos])
        for k in range(cfg.NIT + 1):
            ph.memset('dve', pow2[:, k:k + 1], 2.0 ** (-k), [pow2])
        ph.memset('dve', epsc[:], 1e-5, [epsc])
        rb = ph.sb([32, 8], F32, 'rb')
        oh = ph.sb([32, RR], F32, 'oh')
        vsb = ph.sb([8, RR], F32, 'vsb')
        hk = ph.sb([128, 8, 128], F32, 'hk')
        pv = ph.ps([128, 512], F32, 'pv')
        pt2 = ph.ps([128, 2, 512], F32, 'pt2')
        ph.dma('sp', rb[:], I['relb'], [], [rb], 'rb')
        ph.dma('sp', oh[:], I['ohr'], [], [oh], 'oh')
        ph.mm(pv[0:8, 0:RR], rb[:], oh[:], True, True, [rb, oh], [pv])
        ph.ts('dve', vsb[:], pv[0:8, 0:RR], pv[0:8, RR - 1:RR], None, ALU.subtract, None, [pv], [vsb])
        ph.dma('sp', VB.ap(), vsb[:], [vsb], [VB], 'vb')
        for (tz, base) in ((tz0, 0), (tz1, 128), (tzm, 16)):
            src = bass.AP(tensor=VB, offset=base, ap=[[1, 128], [RR, 8], [1, 128]])
            ph.dma('sp', hk[:], src, [VB], [hk], 'hk')
            for hh in range(2):
                ph.mm(pt2[:, hh, :], jf[:], hk[:, 4 * hh:4 * hh + 4, :], True, True, [jf, hk], [(pt2, hh)])
            ph.copy('dve', tz[:], pt2[:], [(pt2, 0), (pt2, 1)], [tz])
        ph.finish()

        done = phase_done()
        li_a = 0
        li_b = 0
        for layer in range(DEPTH):
            if done:
                break
            isA = (layer % 2 == 0)
            j = layer // 2
            done = done or phase_cache(nc, cfg, I, streams[1:], isA, j, ident_f, phase_done)
            if done:
                break
            done = phase_p1(nc, cfg, I, O, streams, layer, isA, j, ident_f, x0_src, phase_done)
            if done:
                break
            if isA:
                done = phase_attn_a(nc, cfg, streams, ident_b, i30k, vm, vmpos, pow2, tz0, tz1, tzm, phase_done)
            else:
                done = phase_attn_b(nc, cfg, streams, negtri, negones, phase_done)
            if done:
                break
            done = phase_p2b(nc, cfg, I, streams, layer, isA, j, ident_b, epsc, x0_src, phase_done)
            if done:
                break
            done = phase_p3(nc, cfg, I, O, FO, streams, layer, ident_f, epsc, phase_done)
    return nc


def phase_cache(nc, cfg, I, sstreams, isA, j, ident_f, phase_done):
    ph = Phase(nc, 'cache')
    nb = cfg.PAST // 128
    kin = [ph.sb([128, 1024], F32, 'kin') for _ in range(2)]
    vin = [ph.sb([128, 1024], F32, 'vin') for _ in range(2)]
    kts = [ph.sb([128, 8, 128], BF16, 'kts') for _ in range(2)]
    vbs = [ph.sb([128, 1024], BF16, 'vbs') for _ in range(2)]
    pk = [ph.ps([128, 4, 128], F32, 'pk') for _ in range(4)]
    if isA:
        iin = [ph.sb([128, 64], F32, 'iin') for _ in range(2)]
        its = [ph.sb([64, 128], BF16, 'its') for _ in range(2)]
        pi = [ph.ps([128, 128], F32, 'pi') for _ in range(2)]
    bi = 0
    for ss in sstreams:
        ck = ss.ck[isA]
        cv = ss.cv[isA]
        for b in range(nb):
            s = bi % 2
            bi += 1
            ph.dma('sp', kin[s][:], ck[j, b * 128:(b + 1) * 128, :], [], [kin[s]], 'kin%d' % s)
            ph.dma('sp', vin[s][:], cv[j, b * 128:(b + 1) * 128, :], [], [vin[s]], 'vin%d' % s)
            if isA:
                ph.dma('sp', iin[s][:], ss.cik[j, b * 128:(b + 1) * 128, :], [], [iin[s]], 'iin%d' % s)
            for hh in range(2):
                p = pk[2 * s + hh]
                for h4 in range(4):
                    h = 4 * hh + h4
                    ph.tr(p[:, h4, :], kin[s][:, h * 128:(h + 1) * 128], ident_f[:], [kin[s], ident_f], [p])
                ph.evac(kts[s][:, 4 * hh:4 * hh + 4, :], p[:], [p], [kts[s]])
            ph.dma('pool', ss.KT[:, :, b * 128:(b + 1) * 128].rearrange("h d n -> d h n"), kts[s][:], [kts[s]], [], 'kts%d' % s)
            ph.copy('pool', vbs[s][:], vin[s][:], [vin[s]], [vbs[s]])
            ph.dma('pool', ss.V[:, :, b, :].rearrange("h l d -> l h d"), vbs[s][:].rearrange("l (h d) -> l h d", h=8), [vbs[s]], [], 'vbs%d' % s)
            if isA:
                ph.tr(pi[s][0:64, :], iin[s][:, :], ident_f[:], [iin[s], ident_f], [pi[s]])
                ph.evac(its[s][:], pi[s][0:64, :], [pi[s]], [its[s]])
                ph.dma('pool', ss.KIT[:, b * 128:(b + 1) * 128], its[s][:], [its[s]], [], 'its%d' % s)
    ph.finish()
    return phase_done()


CW = 704


def load_weight_bf16(ph, wsb, wdram, nk, ncols, stg, chanp):
    i = 0
    for c in range(nk):
        for c0 in range(0, ncols, CW):
            w = min(CW, ncols - c0)
            s = i % len(stg)
            ph.dma('sp', stg[s][:, 0:w], wdram[c * 128:(c + 1) * 128, c0:c0 + w], [], [stg[s]], '%s%d' % (chanp, s))
            eng = ('dve', 'pool', 'act')[i % 3]
            ph.copy(eng, wsb[:, c, c0:c0 + w], stg[s][:, 0:w], [stg[s]], [wsb])
            i += 1


def phase_p1(nc, cfg, I, O, streams, layer, isA, j, ident_f, x0_src, phase_done):
    ph = Phase(nc, 'p1')
    D = cfg.D
    NIN = cfg.AIN if isA else cfg.BIN
    W = (I['wain'] if isA else I['wbin'])[j]
    wsb = ph.sb([128, 8, NIN], BF16, 'w')
    stg = [ph.sb([128, CW], F32, 'wst') for _ in range(3)]
    load_weight_bf16(ph, wsb, W, 8, NIN, stg, 'wst')
    xin = [ph.sb([128, 4, 1024], F32, 'xin') for _ in range(2)]
    xT = ph.sb([128, 8, 512], BF16, 'xT')
    qts = ph.sb([128, 8, 512], BF16, 'qts')
    kts = ph.sb([128, 8, 512], BF16, 'kts')
    kout = [ph.sb([128, 1024], F32, 'kout') for _ in range(2)]
    vout = [ph.sb([128, 1024], F32, 'vout') for _ in range(2)]
    vbf = [ph.sb([128, 1024], BF16, 'vbf') for _ in range(2)]
    if isA:
        qis = ph.sb([128, 4, 512], BF16, 'qis')
        kis = ph.sb([64, 512], BF16, 'kis')
        kio = [ph.sb([128, 64], F32, 'kio') for _ in range(2)]
        wis = [ph.sb([128, 8], F32, 'wis') for _ in range(2)]
    pT = [ph.ps([128, 512], F32, 'pT') for _ in range(2)]
    pF = [ph.ps([128, 512], F32, 'pF') for _ in range(2)]
    pK = [ph.ps([128, 512], F32, 'pK') for _ in range(4)]
    ph.init_split(cfg, I, 2)

    work = []
    for si, s in enumerate(streams):
        for g in s.groups:
            work.append((si, s, g))

    def load(gi):
        si, s, g = work[gi]
        sl = gi % 2
        for t, qt in enumerate(g):
            q0, n = s.qtiles[qt]
            src = x0_src(s, q0, n) if layer == 0 else s.xrows(q0, n)
            ph.dma('sp', xin[sl][0:n, t, :], src, [], [xin[sl]], 'xin%d' % sl)

    load(0)
    tcount = 0
    for gi, (si, s, g) in enumerate(work):
        sl = gi % 2
        if gi + 1 < len(work):
            load(gi + 1)
        cols = []
        co = 0
        for qt in g:
            q0, n = s.qtiles[qt]
            cols.append((co, n, q0))
            co += n
        N = co
        gq0 = s.qtiles[g[0]][0]
        for c in range(8):
            p = pT[c % 2]
            for t, (co_t, n, q0) in enumerate(cols):
                ph.tr(p[:, co_t:co_t + n], xin[sl][0:n, t, c * 128:(c + 1) * 128], ident_f[0:n, 0:n], [xin[sl], ident_f], [p])
            ph.evac(xT[:, c, 0:N], p[:, 0:N], [p], [xT])
        for m in range(8):
            p = pF[m % 2]
            for c in range(8):
                ph.mm(p[:, 0:N], wsb[:, c, m * 128:(m + 1) * 128], xT[:, c, 0:N], c == 0, c == 7, [wsb, xT], [p])
            ph.act(qts[:, m, 0:N], p[:, 0:N], AF.Copy, [p], [qts], scale=128.0 ** -0.5)
        ph.dma('pool', s.QT[:, :, gq0:gq0 + N].rearrange("h d n -> d h n"), qts[:, :, 0:N], [qts], [], 'qts')
        for m in range(8):
            p = pF[m % 2]
            for c in range(8):
                ph.mm(p[:, 0:N], wsb[:, c, 1024 + m * 128:1024 + (m + 1) * 128], xT[:, c, 0:N], c == 0, c == 7, [wsb, xT], [p])
            ph.evac(kts[:, m, 0:N], p[:, 0:N], [p], [kts])
        ph.dma('pool', s.KT[:, :, s.koff + gq0:s.koff + gq0 + N].rearrange("h d n -> d h n"), kts[:, :, 0:N], [kts], [], 'kts')
        if isA:
            for m in range(4):
                p = pF[m % 2]
                for c in range(8):
                    ph.mm(p[:, 0:N], wsb[:, c, 3072 + m * 128:3072 + (m + 1) * 128], xT[:, c, 0:N], c == 0, c == 7, [wsb, xT], [p])
                ph.evac(qis[:, m, 0:N], p[:, 0:N], [p], [qis])
            ph.dma('pool', s.QIT[:, gq0:gq0 + N].rearrange("(m p) n -> p m n", p=128), qis[:, :, 0:N], [qis], [], 'qis')
            p = pF[0]
            for c in range(8):
                ph.mm(p[0:64, 0:N], wsb[:, c, 3584:3648], xT[:, c, 0:N], c == 0, c == 7, [wsb, xT], [p])
            ph.evac(kis[:, 0:N], p[0:64, 0:N], [p], [kis])
            ph.dma('pool', s.KIT[:, s.koff + gq0:s.koff + gq0 + N], kis[:, 0:N], [kis], [], 'kis')
        for t, (co_t, n, q0) in enumerate(cols):
            ts_ = tcount % 2
            tcount += 1
            kb = s.own[g[t]]
            for half in range(2):
                p = pK[half]
                for c in range(8):
                    ph.mm(p[0:n, :], xT[:, c, co_t:co_t + n], wsb[:, c, 1024 + half * 512:1024 + (half + 1) * 512], c == 0, c == 7, [xT, wsb], [p])
                ph.evac(kout[ts_][0:n, half * 512:(half + 1) * 512], p[0:n, :], [p], [kout[ts_]])
            ph.store_rows(cfg, s, lambda r0, s=s, n=n: s.ko[isA][j, r0:r0 + n, :], q0, n, kout[ts_][0:n, :], [kout[ts_]], 'kout%d' % ts_, key='k')
            for half in range(2):
                p = pK[2 + half]
                for c in range(8):
                    ph.mm(p[0:n, :], xT[:, c, co_t:co_t + n], wsb[:, c, 2048 + half * 512:2048 + (half + 1) * 512], c == 0, c == 7, [xT, wsb], [p])
                ph.evac(vout[ts_][0:n, half * 512:(half + 1) * 512], p[0:n, :], [p], [vout[ts_]])
            ph.copy('pool', vbf[ts_][0:n, :], vout[ts_][0:n, :], [vout[ts_]], [vbf[ts_]])
            ph.dma('pool', s.V[:, 0:n, kb, :].rearrange("h l d -> l h d"), vbf[ts_][0:n, :].rearrange("l (h d) -> l h d", h=8), [vbf[ts_]], [], 'vbf%d' % ts_)
            ph.store_rows(cfg, s, lambda r0, s=s, n=n: s.vo[isA][j, r0:r0 + n, :], q0, n, vout[ts_][0:n, :], [vout[ts_]], 'vout%d' % ts_, key='v')
            if isA:
                p = pT[0]
                for c in range(8):
                    ph.mm(p[0:n, 0:72], xT[:, c, co_t:co_t + n], wsb[:, c, 3584:3656], c == 0, c == 7, [xT, wsb], [p])
                ph.copy('dve', kio[ts_][0:n, :], p[0:n, 0:64], [p], [kio[ts_]])
                ph.act(wis[ts_][0:n, :], p[0:n, 64:72], AF.Copy, [p], [wis[ts_]], scale=512.0 ** -0.5)
                ph.store_rows(cfg, s, lambda r0, s=s, n=n: s.io[j, r0:r0 + n, :], q0, n, kio[ts_][0:n, :], [kio[ts_]], 'kio%d' % ts_, key='i')
                ph.dma('pool', s.WI[q0:q0 + n, :], wis[ts_][0:n, :], [wis[ts_]], [], 'wis%d' % ts_)
    ph.finish()
    return phase_done()


def kv_chunks(s, nblocks, CHB=16):
    out = []
    b = 0
    while b < nblocks:
        e = min(nblocks, b + CHB)
        out.append((b, e))
        b = e
    return out


def load_kv(ph, s, h, b0, b1, ktb, vab, sl):
    p0 = s.kblocks[b0][0]
    p1 = s.kblocks[b1 - 1][0] + s.kblocks[b1 - 1][1]
    ph.dma('sp', ktb[sl][:, 0:p1 - p0], s.KT[h, :, p0:p1], [], [ktb[sl]], 'ktb%d' % sl)
    b = b0
    while b < b1:
        n = s.kblocks[b][1]
        e = b + 1
        while e < b1 and s.kblocks[e][1] == n:
            e += 1
        ph.dma('sp', vab[sl][0:n, b - b0:e - b0, 0:128], s.V[h, 0:n, b:e, :], [], [vab[sl]], 'vab%d' % sl)
        b = e


def phase_attn_a(nc, cfg, streams, ident_b, i30k, vm, vmpos, pow2, tz0, tz1, tzm, phase_done):
    ph = Phase(nc, 'aa')
    NIT = cfg.NIT
    NKmax = max(s.NK for s in streams)
    NKBmax = max(len(s.kblocks) for s in streams)
    kit = ph.sb([64, NKmax], BF16, 'kit')
    maskT = ph.sb([128, NKBmax, 512], BF16, 'maskT')
    score = ph.sb([128, NKmax], F32, 'score')
    selm = ph.sb([128, NKmax], BF16, 'selm')
    qit = [ph.sb([64, 8, 128], BF16, 'qit') for _ in range(2)]
    wi = [ph.sb([128, 8], F32, 'wi') for _ in range(2)]
    qT = ph.sb([128, 8, 512], BF16, 'qT')
    ktb = [ph.sb([128, 2048], BF16, 'ktb') for _ in range(2)]
    vab = [ph.sb([128, 16, 129], BF16, 'vab') for _ in range(2)]
    rsb = [ph.sb([128, 2, 512], F32, 'rsb') for _ in range(3)]
    PT = [ph.sb([128, 512], BF16, 'PT') for _ in range(3)]
    osb = ph.sb([128, 4, 1024], BF16, 'osb')
    sm = ph.sb([128, 16], F32, 'sm')
    htab = ph.sb([128, NIT + 1], F32, 'htab')
    dtmp = ph.sb([128, 128], F32, 'dtmp')
    rc = ph.sb([128, 4], F32, 'rc')
    pI = ph.ps([128, 2, 512], F32, 'pI')
    pL = [ph.ps([128, 512], F32, 'pL') for _ in range(2)]
    pO = [ph.ps([128, 512], F32, 'pO') for _ in range(4)]
    for v_ in vab:
        ph.memset('pool', v_[:, :, 128:129], 1.0, [v_])
    zb = ph.sb([128, 512], BF16, 'zb')
    ph.memset('pool', zb[:], 0.0, [zb])
    MID, CNT, TT, HI, LO, LO2, THR, H0 = [sm[:, i:i + 1] for i in range(8)]

    for s in streams:
        topk = float(s.topk)
        ph.dma('sp', kit[:, 0:s.NK], s.KIT[:, :], [], [kit], 'kit')
        qtc = 0
        for g in s.groups:
            jmax = max(s.own[qt] for qt in g)
            nblk = jmax + 1
            cols = []
            co = 0
            for qt in g:
                cols.append(co)
                co += s.qtiles[qt][1]
            N = co
            gq0 = s.qtiles[g[0]][0]
            ph.dma('sp', qT[:, :, 0:N], s.QT[:, :, gq0:gq0 + N].rearrange("h d n -> d h n"), [], [qT], 'qT')
            for t, qt in enumerate(g):
                q0, nq = s.qtiles[qt]
                own = s.own[qt]
                L = s.kblocks[own][0] + s.kblocks[own][1]
                sl = qtc % 2
                qtc += 1
                ph.dma('sp', qit[sl][:, :, 0:nq], s.QIT[:, q0:q0 + nq].rearrange("(h d) n -> d h n", d=64), [], [qit[sl]], 'qit%d' % sl)
                ph.dma('sp', wi[sl][0:nq, :], s.WI[q0:q0 + nq, :], [], [wi[sl]], 'wi%d' % sl)
                ci = 0
                for c0 in range(0, L, 512):
                    w = min(512, L - c0)
                    for hh in range(4):
                        rs = rsb[ci % 3]
                        ci += 1
                        for h2 in range(2):
                            h = 2 * hh + h2
                            ph.mm(pI[0:nq, h2, 0:w], qit[sl][:, h, 0:nq], kit[:, c0:c0 + w], True, True, [qit[sl], kit], [(pI, h2)])
                        ph.act(rs[0:nq, :, 0:w], pI[0:nq, :, 0:w], AF.Relu, [(pI, 0), (pI, 1)], [rs])
                        for h2 in range(2):
                            h = 2 * hh + h2
                            if h == 0:
                                ph.ts('dve', score[0:nq, c0:c0 + w], rs[0:nq, 0, 0:w], wi[sl][0:nq, 0:1], None, ALU.mult, None, [rs, wi[sl]], [score])
                            else:
                                ph.stt(score[0:nq, c0:c0 + w], rs[0:nq, h2, 0:w], wi[sl][0:nq, h:h + 1], score[0:nq, c0:c0 + w], ALU.mult, ALU.add, [rs, wi[sl], score], [score])
                diag = s.chunked and nq == 128
                if diag:
                    ph.tt('dve', score[:, L - 128:L], score[:, L - 128:L], vm[:], ALU.add, [score, vm], [score])
                if L <= s.topk:
                    ph.memset('dve', THR[0:nq], -BIG / 2, [sm])
                else:
                    ph.op('dve', lambda e, nq=nq, L=L: e.tensor_reduce(out=HI[0:nq], in_=score[0:nq, 0:L], axis=AX.X, op=ALU.max), [score], [sm])
                    if diag:
                        ph.op('dve', lambda e, L=L: e.tensor_reduce(out=LO[:], in_=score[:, 0:L - 128], axis=AX.X, op=ALU.min), [score, sm], [sm])
                        ph.tt('dve', dtmp[:], score[:, L - 128:L], vmpos[:], ALU.add, [score, vmpos], [dtmp])
                        ph.op('dve', lambda e: e.tensor_reduce(out=LO2[:], in_=dtmp[:], axis=AX.X, op=ALU.min), [dtmp, sm], [sm])
                        ph.tt('dve', LO[:], LO[:], LO2[:], ALU.min, [sm], [sm])
                    else:
                        ph.op('dve', lambda e, nq=nq, L=L: e.tensor_reduce(out=LO[0:nq], in_=score[0:nq, 0:L], axis=AX.X, op=ALU.min), [score, sm], [sm])
                    ph.stt(MID[0:nq], HI[0:nq], 1.0, LO[0:nq], ALU.mult, ALU.add, [sm], [sm])
                    ph.ts('dve', MID[0:nq], MID[0:nq], 0.5, None, ALU.mult, None, [sm], [sm])
                    ph.stt(H0[0:nq], LO[0:nq], -1.0, HI[0:nq], ALU.mult, ALU.add, [sm], [sm])
                    ph.ts('dve', H0[0:nq], H0[0:nq], 0.5, None, ALU.mult, None, [sm], [sm])
                    ph.ts('dve', htab[0:nq, :], pow2[0:nq, :], H0[0:nq], None, ALU.mult, None, [sm, pow2], [htab])
                    for k in range(NIT):
                        ph.ts('dve', selm[0:nq, 0:L], score[0:nq, 0:L], MID[0:nq], 0.0, ALU.is_ge, ALU.add, [score, sm], [selm, sm], accum=CNT[0:nq])
                        ph.ts('dve', TT[0:nq], CNT[0:nq], topk - 0.5, -0.5, ALU.is_ge, ALU.add, [sm], [sm])
                        ph.stt(MID[0:nq], TT[0:nq], htab[0:nq, k:k + 1], MID[0:nq], ALU.mult, ALU.add, [sm, htab], [sm])
                    ph.tt('dve', THR[0:nq], MID[0:nq], htab[0:nq, NIT:NIT + 1], ALU.subtract, [sm, htab], [sm])
                ph.ts('dve', selm[0:nq, 0:L], score[0:nq, 0:L], THR[0:nq], -1.0, ALU.is_ge, ALU.add, [score, sm], [selm])
                b = 0
                ti = 0
                while b <= own:
                    nk = s.kblocks[b][1]
                    e_ = b + 1
                    while e_ <= own and e_ - b < 4 and s.kblocks[e_][1] == nk:
                        e_ += 1
                    p = pL[ti % 2]
                    ti += 1
                    pb = p[:].bitcast(BF16)
                    for i in range(b, e_):
                        k0 = s.kblocks[i][0]
                        ph.tr(pb[0:nk, (i - b) * 128:(i - b) * 128 + nq], selm[0:nq, k0:k0 + nk], ident_b[0:nq, 0:nq], [selm, ident_b], [p])
                    src = pb[0:nk, 0:(e_ - b) * 128].rearrange("p (b q) -> p b q", q=128)[:, :, 0:nq]
                    ph.evac(maskT[0:nk, b:e_, cols[t]:cols[t] + nq], src, [p], [maskT])
                    b = e_
                if own < jmax:
                    ph.memset('pool', maskT[:, own + 1:jmax + 1, cols[t]:cols[t] + nq], -1.0, [maskT])
            chunks = kv_chunks(s, nblk)
            lc = 0
            pti = 0
            for h in range(8):
                for (b0, b1) in chunks:
                    sl = lc % 2
                    lc += 1
                    load_kv(ph, s, h, b0, b1, ktb, vab, sl)
                    pbase = s.kblocks[b0][0]
                    for kb in range(b0, b1):
                        k0, nk = s.kblocks[kb]
                        p = pL[pti % 2]
                        pt = PT[pti % 3]
                        pti += 1
                        extra = []
                        for t, qt in enumerate(g):
                            own = s.own[qt]
                            nq = s.qtiles[qt][1]
                            if kb > own:
                                continue
                            if s.chunked:
                                if kb == own:
                                    extra.append((t, nq, tz0))
                                elif kb == own - 1 and nk == 128:
                                    extra.append((t, nq, tz1))
                                elif kb == 0 and own == 1:
                                    extra.append((t, nq, tzm))
                            else:
                                if kb == own:
                                    extra.append((t, nq, tz0))
                                elif kb == own - 1:
                                    extra.append((t, nq, tz1))
                        ph.mm(p[0:nk, 0:N], ktb[sl][:, k0 - pbase:k0 - pbase + nk], qT[:, h, 0:N], True, False, [ktb[sl], qT], [p])
                        ph.mm(p[0:nk, 0:N], i30k[0:nk, 0:nk], maskT[0:nk, kb, 0:N], False, len(extra) == 0, [i30k, maskT], [p])
                        for ei, (t, nq, tz) in enumerate(extra):
                            ph.mm(p[0:nk, cols[t]:cols[t] + nq], ident_b[0:nk, 0:nk], tz[0:nk, h, 0:nq], False, ei == len(extra) - 1, [ident_b, tz], [p])
                        ph.act(pt[0:nk, 0:N], p[0:nk, 0:N], AF.Exp, [p], [pt])
                        for t, qt in enumerate(g):
                            own = s.own[qt]
                            nq = s.qtiles[qt][1]
                            if kb > own:
                                continue
                            po = pO[t]
                            ph.mm(po[0:nq, 0:129], pt[0:nk, cols[t]:cols[t] + nq], vab[sl][0:nk, kb - b0, :], kb == 0, kb == own, [pt, vab[sl]], [po])
                for t, qt in enumerate(g):
                    nq = s.qtiles[qt][1]
                    po = pO[t]
                    ph.op('dve', lambda e, nq=nq, t=t, po=po: e.reciprocal(out=rc[0:nq, t:t + 1], in_=po[0:nq, 128:129]), [po], [(rc, t)])
                    ph.ts('dve', osb[0:nq, t, h * 128:(h + 1) * 128], po[0:nq, 0:128], rc[0:nq, t:t + 1], None, ALU.mult, None, [po, (rc, t)], [osb])
            for t, qt in enumerate(g):
                q0, nq = s.qtiles[qt]
                ph.dma('pool', s.oarows(q0, nq), osb[0:nq, t, :], [osb], [], 'osb')
    ph.finish()
    return phase_done()


def phase_attn_b(nc, cfg, streams, negtri, negones, phase_done):
    ph = Phase(nc, 'ab')
    cm = ph.sb([128, 4, 512], BF16, 'cm')
    ph.memset('pool', cm[:], 1.0, [cm])
    for r in range(4):
        if r > 0:
            ph.memset('pool', cm[:, r, 0:128 * r], 0.0, [cm])
        sl_ = cm[:, r, 128 * r:128 * (r + 1)]
        ph.op('pool', lambda e, sl_=sl_: e.affine_select(out=sl_, in_=sl_, pattern=[[1, 128]], compare_op=ALU.is_gt, fill=0.0, base=0, channel_multiplier=-1), [cm], [cm])
    qT = ph.sb([128, 8, 512], BF16, 'qT')
    ktb = [ph.sb([128, 2048], BF16, 'ktb') for _ in range(2)]
    vab = [ph.sb([128, 16, 129], BF16, 'vab') for _ in range(2)]
    esb = [ph.sb([128, 512], F32, 'esb') for _ in range(2)]
    spb = [ph.sb([128, 512], BF16, 'spb') for _ in range(2)]
    aT = [ph.sb([128, 512], BF16, 'aT') for _ in range(2)]
    sacc = ph.sb([128, 512], BF16, 'sacc')
    osb = ph.sb([128, 4, 1024], BF16, 'osb')
    pZ = [ph.ps([128, 512], F32, 'pZ') for _ in range(2)]
    pD = [ph.ps([128, 512], F32, 'pD') for _ in range(2)]
    pO = [ph.ps([128, 512], F32, 'pO') for _ in range(4)]
    zb = ph.sb([128, 512], BF16, 'zb')
    ph.memset('pool', zb[:], 0.0, [zb])
    onec = ph.sb([128, 1], F32, 'onec')
    ph.memset('pool', onec[:], 1.0, [onec])
    for s in streams:
        for g in s.groups:
            jmax = max(s.own[qt] for qt in g)
            jmin = min(s.own[qt] for qt in g)
            nblk = jmax + 1
            cols = []
            co = 0
            for qt in g:
                cols.append(co)
                co += s.qtiles[qt][1]
            N = co
            gq0 = s.qtiles[g[0]][0]
            ph.dma('sp', qT[:, :, 0:N], s.QT[:, :, gq0:gq0 + N].rearrange("h d n -> d h n"), [], [qT], 'qT')
            bmin = 0
            if cfg.BWIN is not None:
                bmin = max(0, jmin - cfg.BWIN)
            chunks = kv_chunks(s, nblk)
            chunks = [(max(b0, bmin), b1) for (b0, b1) in chunks if b1 > bmin]
            lc = 0
            it = 0
            for h in range(8):
                ph.memset('pool', sacc[:], 0.0, [sacc])
                first = True
                for (b0, b1) in reversed(chunks):
                    sl = lc % 2
                    lc += 1
                    load_kv(ph, s, h, b0, b1, ktb, vab, sl)
                    pbase = s.kblocks[b0][0]
                    for kb in range(b1 - 1, b0 - 1, -1):
                        k0, nk = s.kblocks[kb]
                        z = pZ[it % 2]
                        d = pD[it % 2]
                        es = esb[it % 2]
                        spt = spb[it % 2]
                        at = aT[it % 2]
                        it += 1
                        kt = ktb[sl][:, k0 - pbase:k0 - pbase + nk]
                        ph.mm(z[0:nk, 0:N], kt, qT[:, h, 0:N], True, True, [ktb[sl], qT], [z])
                        ph.act(es[0:nk, 0:N], z[0:nk, 0:N], AF.Exp, [z], [es])
                        ph.act(spt[0:nk, 0:N], es[0:nk, 0:N], AF.Ln, [es, onec], [spt], bias=onec[0:nk, 0:1])
                        ingroup = kb >= jmin
                        if ingroup:
                            cmt = cm[0:nk, kb - jmin, 0:N]
                            ph.tt('pool', spt[0:nk, 0:N], spt[0:nk, 0:N], cmt, ALU.mult, [spt, cm], [spt])
                        ph.mm(d[0:nk, 0:N], kt, qT[:, h, 0:N], True, False, [ktb[sl], qT], [d])
                        ph.mm(d[0:nk, 0:N], negtri[0:nk, 0:nk], spt[0:nk, 0:N], False, first, [negtri, spt], [d])
                        if not first:
                            ph.mm(d[0:nk, 0:N], negones[:, 0:nk], sacc[:, 0:N], False, True, [negones, sacc], [d])
                        ph.act(at[0:nk, 0:N], d[0:nk, 0:N], AF.Exp, [d], [at])
                        if ingroup:
                            ph.tt('pool', at[0:nk, 0:N], at[0:nk, 0:N], cmt, ALU.mult, [at, cm], [at])
                        ph.tt('dve', sacc[0:nk, 0:N], sacc[0:nk, 0:N], spt[0:nk, 0:N], ALU.add, [sacc, spt], [sacc])
                        first = False
                        for t, qt in enumerate(g):
                            own = s.own[qt]
                            nq = s.qtiles[qt][1]
                            if kb > own:
                                continue
                            ph.mm(pO[t][0:nq, 0:128], at[0:nk, cols[t]:cols[t] + nq], vab[sl][0:nk, kb - b0, 0:128], kb == own, kb == bmin, [at, vab[sl]], [pO[t]])
                for t, qt in enumerate(g):
                    nq = s.qtiles[qt][1]
                    ph.evac(osb[0:nq, t, h * 128:(h + 1) * 128], pO[t][0:nq, 0:128], [pO[t]], [osb])
            for t, qt in enumerate(g):
                q0, nq = s.qtiles[qt]
                ph.dma('pool', s.oarows(q0, nq), osb[0:nq, t, :], [osb], [], 'osb')
    ph.finish()
    return phase_done()


def layer_norm(ph, n, t_, out, gsb, bsb, stats, mv, epsc, tmp):
    for hf in range(2):
        ph.op('dve', lambda e, hf=hf: e.bn_stats(out=stats[0:n, hf * 6:(hf + 1) * 6], in_=t_[0:n, hf * 512:(hf + 1) * 512]), [t_], [(stats, hf)])
    ph.op('dve', lambda e: e.bn_aggr(out=mv[0:n, 0:2], in_=stats[0:n, 0:12]), [(stats, 0), (stats, 1)], [mv])
    ph.act(mv[0:n, 2:3], mv[0:n, 1:2], AF.Sqrt, [mv, epsc], [mv], bias=epsc[0:n, 0:1])
    ph.op('dve', lambda e: e.reciprocal(out=mv[0:n, 2:3], in_=mv[0:n, 2:3]), [mv], [mv])
    ph.stt(mv[0:n, 3:4], mv[0:n, 0:1], -1.0, mv[0:n, 2:3], ALU.mult, ALU.mult, [mv], [mv])
    ph.act(tmp[0:n, :], t_[0:n, :], AF.Identity, [t_, mv], [tmp], bias=mv[0:n, 3:4], scale=mv[0:n, 2:3])
    ph.tt('pool', tmp[0:n, :], tmp[0:n, :], gsb[0:n, :], ALU.mult, [tmp, gsb], [tmp])
    ph.tt('dve', out[0:n, :], tmp[0:n, :], bsb[0:n, :], ALU.add, [tmp, bsb], [out])


def phase_p2b(nc, cfg, I, streams, layer, isA, j, ident_b, epsc, x0_src, phase_done):
    ph = Phase(nc, 'p2b')
    W = (I['waout'] if isA else I['wbout'])[j]
    wsb = ph.sb([128, 8, 1024], BF16, 'w')
    stg = [ph.sb([128, CW], F32, 'wst') for _ in range(3)]
    load_weight_bf16(ph, wsb, W, 8, 1024, stg, 'wst')
    gsb = ph.sb([128, 1024], F32, 'g')
    bsb = ph.sb([128, 1024], F32, 'b')
    ph.dma('sp', gsb[:], I['ln1g'][layer:layer + 1, :].partition_broadcast(128), [], [gsb], 'g')
    ph.dma('sp', bsb[:], I['ln1b'][layer:layer + 1, :].partition_broadcast(128), [], [bsb], 'b')
    oin = [ph.sb([128, 1024], BF16, 'oin') for _ in range(2)]
    xin = [ph.sb([128, 1024], F32, 'xin') for _ in range(2)]
    oT = [ph.sb([128, 8, 128], BF16, 'oT') for _ in range(2)]
    tsb = [ph.sb([128, 1024], F32, 'tsb') for _ in range(2)]
    tmp = [ph.sb([128, 1024], F32, 'tmp') for _ in range(2)]
    xo = [ph.sb([128, 1024], F32, 'xo') for _ in range(2)]
    stats = [ph.sb([128, 12], F32, 'stats') for _ in range(2)]
    mv = [ph.sb([128, 4], F32, 'mv') for _ in range(2)]
    pT = [ph.ps([128, 512], F32, 'pT') for _ in range(2)]
    pY = [ph.ps([128, 2, 512], F32, 'pY') for _ in range(2)]
    work = []
    for s in streams:
        for (q0, n) in s.qtiles:
            work.append((s, q0, n))

    def load(i):
        s, q0, n = work[i]
        sl = i % 2
        ph.dma('sp', oin[sl][0:n, :], s.oarows(q0, n), [], [oin[sl]], 'oin%d' % sl)
        src = x0_src(s, q0, n) if layer == 0 else s.xrows(q0, n)
        ph.dma('sp', xin[sl][0:n, :], src, [], [xin[sl]], 'xin%d' % sl)

    load(0)
    for i, (s, q0, n) in enumerate(work):
        sl = i % 2
        if i + 1 < len(work):
            load(i + 1)
        pb = pT[sl][:].bitcast(BF16)
        for c in range(8):
            ph.tr(pb[:, c * 128:c * 128 + n], oin[sl][0:n, c * 128:(c + 1) * 128], ident_b[0:n, 0:n], [oin[sl], ident_b], [pT[sl]])
        ph.evac(oT[sl][:, :, 0:n], pb[:, 0:1024].rearrange("p (c q) -> p c q", q=128)[:, :, 0:n], [pT[sl]], [oT[sl]])
        for half in range(2):
            for c in range(8):
                ph.mm(pY[sl][0:n, half, :], oT[sl][:, c, 0:n], wsb[:, c, half * 512:(half + 1) * 512], c == 0, c == 7, [oT[sl], wsb], [(pY[sl], half)])
        ph.stt(tsb[sl][0:n, :], xin[sl][0:n, :], cfg.ALPHA, pY[sl][0:n, :, :].rearrange("p a b -> p (a b)"), ALU.mult, ALU.add, [xin[sl], (pY[sl], 0), (pY[sl], 1)], [tsb[sl]])
        layer_norm(ph, n, tsb[sl], xo[sl], gsb, bsb, stats[sl], mv[sl], epsc, tmp[sl])
        ph.dma('pool', s.xrows(q0, n), xo[sl][0:n, :], [xo[sl]], [], 'xo%d' % sl)
    ph.finish()
    return phase_done()


def phase_p3(nc, cfg, I, O, FO, streams, layer, ident_f, epsc, phase_done):
    ph = Phase(nc, 'p3')
    D, DFF, NF = cfg.D, cfg.DFF, cfg.NF
    last = (layer == cfg.DEPTH - 1)
    wg = ph.sb([128, 8, DFF], BF16, 'wg')
    wu = ph.sb([128, 8, DFF], BF16, 'wu')
    wd = ph.sb([128, NF, 1024], BF16, 'wd')
    stg = [ph.sb([128, CW], F32, 'wst') for _ in range(2)]
    load_weight_bf16(ph, wg, I['wg'][layer], 8, DFF, stg, 'wst')
    load_weight_bf16(ph, wu, I['wu'][layer], 8, DFF, stg, 'wst')
    load_weight_bf16(ph, wd, I['wd'][layer], NF, 1024, stg, 'wst')
    gsb = ph.sb([128, 1024], F32, 'g')
    bsb = ph.sb([128, 1024], F32, 'b')
    ph.dma('sp', gsb[:], I['ln2g'][layer:layer + 1, :].partition_broadcast(128), [], [gsb], 'g')
    ph.dma('sp', bsb[:], I['ln2b'][layer:layer + 1, :].partition_broadcast(128), [], [bsb], 'b')
    cw = ph.sb([128, 3, NF], F32, 'cw')
    cb = ph.sb([128, NF], F32, 'cb')
    for r_ in range(3):
        ph.dma('sp', cw[:, r_, :], I['cw'][layer, r_, :].rearrange("(c p) -> p c", p=128), [], [cw], 'cw', slow=True)
    ph.dma('sp', cb[:], I['cb'][layer, :].rearrange("(c p) -> p c", p=128), [], [cb], 'cb', slow=True)
    gprev = ph.sb([128, NF, 2], F32, 'gprev')
    xin = [ph.sb([128, 2, 1024], F32, 'xin') for _ in range(2)]
    xT = ph.sb([128, 8, 256], BF16, 'xT')
    hT = ph.sb([128, NF, 256], BF16, 'hT')
    gs = [ph.sb([128, 258], F32, 'gs') for _ in range(2)]
    ga = [ph.sb([128, 256], F32, 'ga') for _ in range(2)]
    tsb = [ph.sb([128, 1024], F32, 'tsb')] * 2
    tmp = tsb
    xo = [ph.sb([128, 1024], F32, 'xo')] * 2
    stats = [ph.sb([128, 12], F32, 'stats') for _ in range(2)]
    mv = [ph.sb([128, 4], F32, 'mv') for _ in range(2)]
    pT = [ph.ps([128, 512], F32, 'pT') for _ in range(2)]
    pG = [ph.ps([128, 512], F32, 'pG') for _ in range(2)]
    pU = [ph.ps([128, 512], F32, 'pU') for _ in range(2)]
    pY = ph.ps([128, 2, 512], F32, 'pY')
    if last:
        ph.init_split(cfg, I, 1)
    work = []
    for si, s in enumerate(streams):
        for gi, g in enumerate(s.groups2):
            work.append((si, s, gi, g))

    def load(i):
        si, s, gi, g = work[i]
        sl = i % 2
        for t, qt in enumerate(g):
            q0, n = s.qtiles[qt]
            ph.dma('sp', xin[sl][0:n, t, :], s.xrows(q0, n), [], [xin[sl]], 'xin%d' % sl)

    load(0)
    tcount = 0
    for i, (si, s, gi, g) in enumerate(work):
        sl = i % 2
        if i + 1 < len(work):
            load(i + 1)
        cols = []
        co = 0
        for qt in g:
            q0, n = s.qtiles[qt]
            cols.append((co, n, q0))
            co += n
        N = co
        lastg = (gi == len(s.groups2) - 1)
        if gi == 0:
            if s.kind == 'p':
                ph.memset('pool', gprev[:], 0.0, [gprev])
            else:
                for r_ in range(2):
                    ph.dma('sp', gprev[:, :, r_], s.sconv[layer, r_, :].rearrange("(c p) -> p c", p=128), [], [gprev], 'gprev', slow=True)
        for c in range(8):
            p = pT[c % 2]
            for t, (co_t, n, q0) in enumerate(cols):
                ph.tr(p[:, co_t:co_t + n], xin[sl][0:n, t, c * 128:(c + 1) * 128], ident_f[0:n, 0:n], [xin[sl], ident_f], [p])
            ph.evac(xT[:, c, 0:N], p[:, 0:N], [p], [xT])
        for f in range(NF):
            fs = f % 2
            pg = pG[fs]
            pu = pU[fs]
            for c in range(8):
                ph.mm(pg[:, 0:N], wg[:, c, f * 128:(f + 1) * 128], xT[:, c, 0:N], c == 0, c == 7, [wg, xT], [pg])
            for c in range(8):
                ph.mm(pu[:, 0:N], wu[:, c, f * 128:(f + 1) * 128], xT[:, c, 0:N], c == 0, c == 7, [wu, xT], [pu])
            g_ = gs[fs]
            ph.copy('pool', g_[:, 0:2], gprev[:, f, :], [gprev], [g_])
            ph.act(g_[:, 2:2 + N], pg[:, 0:N], AF.Copy, [pg], [g_])
            ph.copy('pool', gprev[:, f, :], g_[:, N:N + 2], [g_], [gprev])
            a_ = ga[fs]
            ph.ts('dve', a_[:, 0:N], g_[:, 2:2 + N], cw[:, 2, f:f + 1], cb[:, f:f + 1], ALU.mult, ALU.add, [g_, cw, cb], [a_])
            ph.stt(a_[:, 0:N], g_[:, 1:1 + N], cw[:, 1, f:f + 1], a_[:, 0:N], ALU.mult, ALU.add, [g_, cw, a_], [a_])
            ph.stt(a_[:, 0:N], g_[:, 0:N], cw[:, 0, f:f + 1], a_[:, 0:N], ALU.mult, ALU.add, [g_, cw, a_], [a_])
            ph.act(a_[:, 0:N], a_[:, 0:N], AF.Gelu_apprx_tanh, [a_], [a_])
            ph.tt('dve', hT[:, f, 0:N], a_[:, 0:N], pu[:, 0:N], ALU.mult, [a_, pu], [hT])
        if lastg:
            for r_ in range(2):
                dst = s.convo[layer, r_, :].rearrange("(c p) -> p c", p=128)
                ph.dma('pool', dst, gprev[:, :, r_], [gprev], [], 'gpo', slow=True)
        for t, (co_t, n, q0) in enumerate(cols):
            ts_ = tcount % 2
            tcount += 1
            for half in range(2):
                for f in range(NF):
                    ph.mm(pY[0:n, half, :], hT[:, f, co_t:co_t + n], wd[:, f, half * 512:(half + 1) * 512], f == 0, f == NF - 1, [hT, wd], [(pY, half)])
            ph.stt(tsb[ts_][0:n, :], xin[sl][0:n, t, :], cfg.ALPHA, pY[0:n, :, :].rearrange("p a b -> p (a b)"), ALU.mult, ALU.add, [xin[sl], (pY, 0), (pY, 1)], [tsb[ts_]])
            layer_norm(ph, n, tsb[ts_], xo[ts_], gsb, bsb, stats[ts_], mv[ts_], epsc, tmp[ts_])
            if not last:
                ph.dma('pool', s.xrows(q0, n), xo[ts_][0:n, :], [xo[ts_]], [], 'xo%d' % ts_)
            else:
                if s.kind == 's':
                    ph.dma('pool', s.yo[q0:q0 + n, :], xo[ts_][0:n, :], [xo[ts_]], [], 'xo%d' % ts_)
                elif q0 >= 16:
                    ph.store_rows(cfg, s, lambda r0, n=n: FO['yp'][r0:r0 + n, :], q0, n, xo[ts_][0:n, :], [xo[ts_]], 'xo%d' % ts_, yrow=True, key='y')
    ph.finish()
    return phase_done()


def phase_combine(nc, cfg, I, O, FO):
    ph = Phase(nc, 'comb')
    mk = ph.sb([128, 2], F32, 'mk')
    ph.dma('sp', mk[:], I['maskf'].partition_broadcast(128), [], [mk], 'mk')
    ta = [ph.sb([128, 1024], F32, 'ta') for _ in range(2)]
    tb = [ph.sb([128, 1024], F32, 'tb') for _ in range(2)]
    to = [ph.sb([128, 1024], F32, 'to') for _ in range(2)]
    items = []
    for j in range(cfg.NA):
        items += [(FO['akp'][j], O['akp'][j], cfg.PH, 1024), (FO['avp'][j], O['avp'][j], cfg.PH, 1024), (FO['aikp'][j], O['aikp'][j], cfg.PH, 64)]
    for j in range(cfg.NB):
        items += [(FO['bkp'][j], O['bkp'][j], cfg.PH, 1024), (FO['bvp'][j], O['bvp'][j], cfg.PH, 1024)]
    items.append((FO['yp'], O['yp'], cfg.PH - 16, 1024))
    work = []
    for (full, out, split, w) in items:
        rfull = full.shape[0]
        nA = split
        nB = rfull - split
        for r0 in range(0, nA, 128):
            n = min(128, nA - r0)
            nb = min(n, max(0, nB - r0))
            work.append((full, out, split, w, r0, n, nb))

    def load(i):
        full, out, split, w, r0, n, nb = work[i]
        sl = i % 2
        ph.dma('sp', ta[sl][0:n, 0:w], full[r0:r0 + n, :], [], [ta[sl]], 'ta%d' % sl)
        if nb > 0:
            ph.dma('sp', tb[sl][0:nb, 0:w], full[split + r0:split + r0 + nb, :], [], [tb[sl]], 'tb%d' % sl)

    load(0)
    for i, (full, out, split, w, r0, n, nb) in enumerate(work):
        sl = i % 2
        if i + 1 < len(work):
            load(i + 1)
        ph.act(to[sl][0:n, 0:w], ta[sl][0:n, 0:w], AF.Copy, [ta[sl], mk], [to[sl]], scale=mk[0:n, 0:1])
        if nb > 0:
            ph.stt(to[sl][0:nb, 0:w], tb[sl][0:nb, 0:w], mk[0:nb, 1:2], to[sl][0:nb, 0:w], ALU.mult, ALU.add, [tb[sl], mk, to[sl]], [to[sl]])
        ph.dma('pool', out[r0:r0 + n, :], to[sl][0:n, 0:w], [to[sl]], [], 'to%d' % sl)
    ph.finish()


_cache = {}


def make_in_maps(cfg, inputs, ncores):
    f = lambda a: np.ascontiguousarray(np.asarray(a, dtype=np.float32))
    B = inputs['x_prompt'].shape[0]
    rel = 127 - np.arange(RR)
    bk = rel_bucket_np(rel)
    ohr = np.zeros((32, RR), np.float32)
    ohr[bk, np.arange(RR)] = 1.0
    NA, NB, NS = cfg.NA, cfg.NB, cfg.NSAMP
    NBm = max(NB, 1)
    shared = {
        'meta': f(inputs['meta_tokens']),
        'relb': f(inputs['rel_bias']),
        'wain': f(inputs['w_a_in']),
        'waout': f(inputs['w_a_out']),
        'wbin': f(inputs['w_b_in']),
        'wbout': f(inputs['w_b_out']),
        'ln1g': f(inputs['ln1_g']), 'ln1b': f(inputs['ln1_b']),
        'ln2g': f(inputs['ln2_g']), 'ln2b': f(inputs['ln2_b']),
        'wg': f(inputs['w_ffn_gate']), 'wu': f(inputs['w_ffn_up']),
        'cw': f(inputs['ffn_conv_w']), 'cb': f(inputs['ffn_conv_b']),
        'wd': f(inputs['w_ffn_down']),
        'ohr': ohr,
    }
    maps = []
    for c in range(ncores):
        bp = c % B
        sl = slice(c * NS, (c + 1) * NS)
        m = dict(shared)
        if cfg.SPLIT:
            half = c // B
            m['maskf'] = np.array([[1 - half, half]], dtype=np.float32)
        m['xp'] = np.concatenate([f(inputs['x_prompt'][bp]).reshape(cfg.S // 2, 2 * cfg.D), np.zeros((1, 2 * cfg.D), np.float32)], 0)
        m['xs'] = f(inputs['x_sample'][sl])
        m['cak'] = f(np.asarray(inputs['cache_a_k'])[:, sl]).transpose(1, 0, 2, 3, 4).reshape(NS, NA, cfg.PAST, cfg.D)
        m['cav'] = f(np.asarray(inputs['cache_a_v'])[:, sl]).transpose(1, 0, 2, 3, 4).reshape(NS, NA, cfg.PAST, cfg.D)
        m['caik'] = np.ascontiguousarray(f(np.asarray(inputs['cache_a_idx_k'])[:, sl]).transpose(1, 0, 2, 3))
        m['cbk'] = f(np.asarray(inputs['cache_b_k'])[:, sl]).transpose(1, 0, 2, 3, 4).reshape(NS, NBm, cfg.PAST, cfg.D)
        m['cbv'] = f(np.asarray(inputs['cache_b_v'])[:, sl]).transpose(1, 0, 2, 3, 4).reshape(NS, NBm, cfg.PAST, cfg.D)
        m['sconv'] = np.ascontiguousarray(f(np.asarray(inputs['state_ffn_conv'])[:, sl]).transpose(1, 0, 2, 3))
        for k in ('cak', 'cav', 'cbk', 'cbv'):
            m[k] = np.ascontiguousarray(m[k])
        maps.append(m)
    return maps


def assemble(cfg, res, B, ncores):
    NA, NB, P, TS, D, NS = cfg.NA, cfg.NB, cfg.P, cfg.TS, cfg.D, cfg.NSAMP
    r = res
    pc = list(range(B))
    g = lambda key, c: np.asarray(r[c][key], dtype=np.float32)
    if cfg.SPLIT:
        def stp(key):
            outs = []
            for b in pc:
                a0, a1 = g(key, b), g(key, b + B)
                if key == 'yp':
                    outs.append(np.concatenate([a0[:cfg.PH - 16], a1[:cfg.P - cfg.PH]], 0))
                elif key == 'convp':
                    outs.append(a0)
                else:
                    outs.append(np.concatenate([a0[:, :cfg.PH], a1[:, :cfg.P - cfg.PH]], 1))
            return np.stack(outs, 0)
    else:
        stp = lambda key: np.stack([g(key, c) for c in pc], 0)
    sts = lambda key: np.concatenate([g(key, c) for c in range(ncores)], 0)
    y_p = stp('yp')
    y_s = sts('ys')
    akp = stp('akp').transpose(1, 0, 2, 3).reshape(NA, B, P, 8, 128)
    avp = stp('avp').transpose(1, 0, 2, 3).reshape(NA, B, P, 8, 128)
    aikp = stp('aikp').transpose(1, 0, 2, 3)
    bkp = stp('bkp').transpose(1, 0, 2, 3)[:NB].reshape(NB, B, P, 8, 128)
    bvp = stp('bvp').transpose(1, 0, 2, 3)[:NB].reshape(NB, B, P, 8, 128)
    convp = stp('convp').transpose(1, 0, 2, 3)
    nsq = ncores * NS
    aks = sts('aks').transpose(1, 0, 2, 3).reshape(NA, nsq, TS, 8, 128)
    avs = sts('avs').transpose(1, 0, 2, 3).reshape(NA, nsq, TS, 8, 128)
    aiks = sts('aiks').transpose(1, 0, 2, 3)
    bks = sts('bks').transpose(1, 0, 2, 3)[:NB].reshape(NB, nsq, TS, 8, 128)
    bvs = sts('bvs').transpose(1, 0, 2, 3)[:NB].reshape(NB, nsq, TS, 8, 128)
    convs = sts('convs').transpose(1, 0, 2, 3)
    return tuple(np.ascontiguousarray(a) for a in (y_p, y_s, akp, avp, aikp, bkp, bvp, convp, aks, avs, aiks, bks, bvs, convs))


def kernel(**inputs):
    cfg = Cfg()
    cfg.NSAMP = 1
    cfg.SPLIT = True
    cfg.ALIAS = True
    ncores = 8
    nc = build(cfg)
    maps = make_in_maps(cfg, inputs, ncores)
    res = run_bass_kernel_spmd(nc, maps, core_ids=list(range(ncores)))
    return assemble(cfg, res.results, inputs['x_prompt'].shape[0], ncores)
```

```python
import contextlib
import math
import numpy as np
import concourse.bass as bass
import concourse.mybir as mybir
from concourse.bass_utils import run_bass_kernel_spmd

F32 = mybir.dt.float32
BF16 = mybir.dt.bfloat16
ALU = mybir.AluOpType
AF = mybir.ActivationFunctionType
AX = mybir.AxisListType

ENGS = ('pe', 'act', 'dve', 'pool', 'sp')
SAME_ENG_SYNC = True


class Op:
    __slots__ = ('eng', 'fn', 'deps', 'is_dma', 'chan', 'cum', 'signal', 'count', 'idx')


class Glob:
    def __init__(self, nc, stack):
        self.nc = nc
        self.stack = stack
        self.eng_sem = {e: stack.enter_context(nc.semaphore('s_' + e)) for e in ENGS}
        self.eng_cnt = {e: 0 for e in ENGS}
        self.chan_sem = {}
        self.chan_cum = {}

    def chan(self, name):
        if name not in self.chan_sem:
            self.chan_sem[name] = self.stack.enter_context(self.nc.semaphore('c_' + name))
            self.chan_cum[name] = 0
        return self.chan_sem[name]


class Sched:
    def __init__(self, glob):
        self.glob = glob
        self.ops = {e: [] for e in ENGS}
        self.state = {}
        self.chans = set()
        self.n = 0

    @staticmethod
    def _key(r):
        if isinstance(r, tuple):
            return (id(r[0]),) + tuple(r[1:])
        return (id(r),)

    def add(self, eng, fn, reads=(), writes=(), dma=False, chan=None):
        op = Op()
        op.eng = eng
        op.fn = fn
        op.is_dma = dma
        op.signal = False
        op.count = None
        op.chan = chan
        op.idx = self.n
        self.n += 1
        deps = {}
        for r in reads:
            st = self.state.get(self._key(r))
            if st is not None and st[0] is not None:
                deps[st[0].idx] = st[0]
        for w in writes:
            st = self.state.get(self._key(w))
            if st is not None:
                w0 = st[0]
                if w0 is not None:
                    if not (dma and w0.is_dma and w0.chan == chan and not st[1] and not st[2]):
                        deps[w0.idx] = w0
                for o in st[1].values():
                    deps[o.idx] = o
                for o in st[2]:
                    deps[o.idx] = o
        op.deps = list(deps.values())
        for r in reads:
            st = self.state.setdefault(self._key(r), [None, {}, []])
            if dma:
                st[2].append(op)
            else:
                st[1][eng] = op
        for w in writes:
            self.state[self._key(w)] = [op, {}, []]
        if dma:
            assert chan is not None
            self.glob.chan(chan)
            self.glob.chan_cum[chan] += 1
            op.cum = self.glob.chan_cum[chan]
            self.chans.add(chan)
        self.ops[eng].append(op)
        return op

    def emit(self, nc, stack, tag):
        G = self.glob
        for e in ENGS:
            for op in self.ops[e]:
                for d in op.deps:
                    if d.is_dma:
                        continue
                    if d.eng == e and (e == 'pe' or not SAME_ENG_SYNC):
                        continue
                    d.signal = True
        eng_sem = G.eng_sem
        for e in ENGS:
            c = G.eng_cnt[e]
            for op in self.ops[e]:
                if op.signal and not op.is_dma:
                    c += 1
                    op.count = c
            G.eng_cnt[e] = c
        chan_sem = G.chan_sem
        final = [(chan_sem[ch], 16 * G.chan_cum[ch]) for ch in sorted(self.chans)]
        sched = self

        def run(e, h):
            known = {}
            for op in sched.ops[e]:
                waits = {}
                for d in op.deps:
                    if d.is_dma:
                        s, v = chan_sem[d.chan], 16 * d.cum
                    else:
                        if d.eng == e and (e == 'pe' or not SAME_ENG_SYNC):
                            continue
                        s, v = eng_sem[d.eng], d.count
                    k = id(s)
                    if k not in waits or waits[k][1] < v:
                        waits[k] = (s, v)
                for k, (s, v) in waits.items():
                    if known.get(k, 0) >= v:
                        continue
                    h.wait_ge(s, v)
                    STATS[tag + ':' + e + ':wait'] = STATS.get(tag + ':' + e + ':wait', 0) + 1
                    known[k] = v
                STATS[tag + ':' + e] = STATS.get(tag + ':' + e, 0) + 1
                ins = op.fn(h)
                if op.is_dma:
                    ins.then_inc(chan_sem[op.chan], 16)
                elif op.signal:
                    ins.then_inc(eng_sem[e], 1)
            if e == 'sp':
                for s, v in final:
                    h.wait_ge(s, v)

        with nc.Block() as block:
            @block.sync
            def _(h):
                run('sp', h)

            @block.tensor
            def _(h):
                run('pe', h)

            @block.scalar
            def _(h):
                run('act', h)

            @block.vector
            def _(h):
                run('dve', h)

            @block.gpsimd
            def _(h):
                run('pool', h)


_uid = [0]
STATS = {}


class Phase:
    glob = None

    def __init__(self, nc, name):
        self.nc = nc
        self.name = name
        self.st = contextlib.ExitStack()
        self.S = Sched(Phase.glob)
        self.rr = 0

    def sb(self, shape, dt, name='t'):
        _uid[0] += 1
        return self.st.enter_context(self.nc.sbuf_tensor('%s_%s%d' % (self.name, name, _uid[0]), list(shape), dt))

    def ps(self, shape, dt=F32, name='p'):
        _uid[0] += 1
        return self.st.enter_context(self.nc.psum_tensor('%s_%s%d' % (self.name, name, _uid[0]), list(shape), dt))

    def finish(self):
        _uid[0] += 1
        self.S.emit(self.nc, self.st, self.name)
        self.st.close()

    def op(self, eng, fn, r=(), w=()):
        return self.S.add(eng, fn, r, w)

    def dma(self, eng, out, in_, r, w, chan, slow=False):
        if eng == 'pool':
            eng = 'sp'
        if slow:
            fn = lambda e: e.dma_start(out=out, in_=in_, allow_slow_non_contiguous=True)
        else:
            fn = lambda e: e.dma_start(out=out, in_=in_)
        return self.S.add(eng, fn, r, w, dma=True, chan=chan)

    def store_rows(self, cfg, s, dst, q0, n, src, r, chan, yrow=False, key=None):
        off = 16 if yrow else 0
        if not (cfg.SPLIT and s.kind == 'p'):
            return self.dma('pool', dst(q0 - off), src, r, [], chan)
        owner = 1 if q0 >= cfg.PH else 0
        row0 = q0 - off - owner * (cfg.PH - off)
        out = dst(row0)
        mk = self.mk
        res = ('dramout', key)
        if owner == 0:
            self.ts('pool', src, src, mk[0:n, 0:1], None, ALU.mult, None, list(r) + [mk], list(r))
            return self.dma('pool', out, src, r, [res], chan)
        self.rmwi = getattr(self, 'rmwi', 0) + 1
        tmp = self.rmw[self.rmwi % len(self.rmw)]
        w = src.shape[-1]
        self.dma('sp', tmp[0:n, 0:w], out, [res], [tmp], 'rmw%d' % (self.rmwi % len(self.rmw)))
        self.stt(src, src, mk[0:n, 1:2], tmp[0:n, 0:w], ALU.mult, ALU.add, list(r) + [mk, tmp], list(r))
        return self.dma('pool', out, src, r, [], chan)

    def init_split(self, cfg, I, nslots):
        if cfg.SPLIT:
            self.mk = self.sb([128, 2], F32, 'mk')
            self.dma('sp', self.mk[:], I['maskf'].partition_broadcast(128), [], [self.mk], 'mk')
            self.rmw = [self.sb([128, 1024], F32, 'rmw') for _ in range(nslots)]

    def mm(self, out, lhsT, rhs, start, stop, r, w):
        return self.S.add('pe', lambda e: e.matmul(out, lhsT=lhsT, rhs=rhs, start=start, stop=stop), r, w)

    def tr(self, out, in_, ident, r, w):
        return self.S.add('pe', lambda e: e.transpose(out=out, in_=in_, identity=ident), r, w)

    def act(self, out, in_, func, r, w, bias=None, scale=None, accum=None):
        kw = {}
        if bias is not None:
            kw['bias'] = bias
        if scale is not None:
            kw['scale'] = scale
        if accum is not None:
            kw['accum_out'] = accum
        return self.S.add('act', lambda e: e.activation(out=out, in_=in_, func=func, **kw), r, w)

    def copy(self, eng, out, in_, r, w):
        if eng == 'act':
            return self.act(out, in_, AF.Copy, r, w)
        return self.S.add(eng, lambda e: e.tensor_copy(out=out, in_=in_), r, w)

    def evac(self, out, in_, r, w):
        self.rr += 1
        return self.copy('act' if self.rr % 2 else 'dve', out, in_, r, w)

    def ts(self, eng, out, in0, s1, s2, op0, op1, r, w, accum=None):
        if op1 is None:
            fn = lambda e: e.tensor_scalar(out=out, in0=in0, scalar1=s1, scalar2=None, op0=op0)
        elif accum is None:
            fn = lambda e: e.tensor_scalar(out=out, in0=in0, scalar1=s1, scalar2=s2, op0=op0, op1=op1)
        else:
            fn = lambda e: e.tensor_scalar(out=out, in0=in0, scalar1=s1, scalar2=s2, op0=op0, op1=op1, accum_out=accum)
        return self.S.add(eng, fn, r, w)

    def stt(self, out, in0, scalar, in1, op0, op1, r, w):
        return self.S.add('dve', lambda e: e.scalar_tensor_tensor(out=out, in0=in0, scalar=scalar, in1=in1, op0=op0, op1=op1), r, w)

    def tt(self, eng, out, in0, in1, op, r, w):
        return self.S.add(eng, lambda e: e.tensor_tensor(out=out, in0=in0, in1=in1, op=op), r, w)

    def memset(self, eng, ap, val, w):
        return self.S.add(eng, lambda e: e.memset(ap, val), (), w)


class Cfg:
    def __init__(self, S=8192, PAST=2048, TS=16, DEPTH=4):
        self.D = 1024
        self.H = 8
        self.HD = 128
        self.DFF = 2816
        self.NF = 22
        self.IH = 8
        self.ID = 64
        self.NM = 16
        self.S = S
        self.PAST = PAST
        self.TS = TS
        self.DEPTH = DEPTH
        self.NA = (DEPTH + 1) // 2
        self.NB = DEPTH // 2
        self.P = self.NM + S
        self.PS = PAST + TS
        self.AIN = 3656
        self.BIN = 3072
        self.TOPK_P = min(256, S // 4)
        self.TOPK_S = min(256, self.PS // 4)
        self.ALPHA = (2 * DEPTH) ** 0.25
        self.NIT = 20
        self.BWIN = 8
        self.upto = None
        self.NSAMP = 1
        self.ALIAS = False
        self.SPLIT = False
        self.PH = 16 + 128 * ((S // 128) // 2)


BIG = 1.0e4
RR = 383


def rel_bucket_np(rel):
    import jax
    import jax.numpy as jnp

    def f(rel):
        nb = 16
        max_exact = 8
        ret = jnp.where(rel > 0, nb, 0)
        n = jnp.abs(rel)
        nf = jnp.maximum(n, 1).astype(jnp.float32)
        large = max_exact + (jnp.log(nf / max_exact) / math.log(128 / max_exact) * (nb - max_exact)).astype(jnp.int32)
        large = jnp.minimum(large, nb - 1)
        return ret + jnp.where(n < max_exact, n, large)

    cpu = jax.devices('cpu')[0]
    with jax.default_device(cpu):
        out = jax.jit(f, backend='cpu')(jax.device_put(np.asarray(rel, dtype=np.int32), cpu))
        return np.asarray(out)


class Stream:
    pass


def make_streams(cfg):
    sp = Stream()
    sp.name = 'p'
    sp.NQ = cfg.P
    sp.NK = cfg.P
    sp.koff = 0
    sp.qtiles = [(0, 16)] + [(16 + 128 * i, 128) for i in range(cfg.S // 128)]
    sp.kblocks = list(sp.qtiles)
    sp.own = list(range(len(sp.qtiles)))
    ng = (len(sp.qtiles) - 1) // 4
    sp.groups = [[0]] + [[1 + 4 * g + i for i in range(4)] for g in range(ng)]
    sp.groups2 = [[0]] + [[1 + 2 * g + i for i in range(2)] for g in range(2 * ng)]
    sp.topk = cfg.TOPK_P
    sp.chunked = True
    sp.kind = 'p'
    sp.sidx = 0
    sss = []
    for i in range(cfg.NSAMP):
        ss = Stream()
        ss.name = 's%d' % i
        ss.kind = 's'
        ss.sidx = i
        ss.NQ = cfg.TS
        ss.NK = cfg.PS
        ss.koff = cfg.PAST
        ss.qtiles = [(0, 16)]
        ss.kblocks = [(128 * b, 128) for b in range(cfg.PAST // 128)] + [(cfg.PAST, 16)]
        ss.own = [len(ss.kblocks) - 1]
        ss.groups = [[0]]
        ss.groups2 = [[0]]
        ss.topk = cfg.TOPK_S
        ss.chunked = False
        sss.append(ss)
    return [sp] + sss


def build(cfg):
    nc = bass.Bass("TRN2", target_bir_lowering=False)
    D, H, DFF, NF = cfg.D, cfg.H, cfg.DFF, cfg.NF
    NA, NB, DEPTH = cfg.NA, cfg.NB, cfg.DEPTH
    streams = make_streams(cfg)
    sp = streams[0]
    NS = cfg.NSAMP

    def din(name, shape, dt=F32):
        return nc.dram_tensor(name, list(shape), dt, kind="ExternalInput").ap()

    def dout(name, shape, dt=F32):
        return nc.dram_tensor(name, list(shape), dt, kind="ExternalOutput").ap()

    def dscr(name, shape, dt):
        if getattr(cfg, 'debug', False):
            return nc.dram_tensor(name, list(shape), dt, kind="ExternalOutput").ap()
        return nc.dram_tensor(name, list(shape), dt).ap()

    I = {}
    I['xp'] = din('xp', [cfg.S // 2 + 1, 2 * D])[0:cfg.S // 2, :].rearrange("a (two d) -> (a two) d", two=2)
    I['xs'] = din('xs', [NS, cfg.TS, D])
    I['meta'] = din('meta', [16, D])
    I['cak'] = din('cak', [NS, NA, cfg.PAST, D])
    I['cav'] = din('cav', [NS, NA, cfg.PAST, D])
    I['caik'] = din('caik', [NS, NA, cfg.PAST, 64])
    I['cbk'] = din('cbk', [NS, max(NB, 1), cfg.PAST, D])
    I['cbv'] = din('cbv', [NS, max(NB, 1), cfg.PAST, D])
    I['sconv'] = din('sconv', [NS, DEPTH, 2, DFF])
    I['relb'] = din('relb', [32, 8])
    I['wain'] = din('wain', [NA, D, cfg.AIN])
    I['waout'] = din('waout', [NA, D, D])
    I['wbin'] = din('wbin', [max(NB, 1), D, cfg.BIN])
    I['wbout'] = din('wbout', [max(NB, 1), D, D])
    for nm in ('ln1g', 'ln1b', 'ln2g', 'ln2b'):
        I[nm] = din(nm, [DEPTH, D])
    I['wg'] = din('wg', [DEPTH, D, DFF])
    I['wu'] = din('wu', [DEPTH, D, DFF])
    I['cw'] = din('cw', [DEPTH, 3, DFF])
    I['cb'] = din('cb', [DEPTH, DFF])
    I['wd'] = din('wd', [DEPTH, DFF, D])
    I['ohr'] = din('ohr', [32, RR])
    if cfg.SPLIT:
        I['maskf'] = din('maskf', [1, 2])
    PR = max(cfg.PH, cfg.P - cfg.PH) if cfg.SPLIT else cfg.P
    YR = max(cfg.PH - 16, cfg.P - cfg.PH) if cfg.SPLIT else cfg.S

    O = {}
    O['yp'] = dout('yp', [YR, D])
    O['ys'] = dout('ys', [NS, cfg.TS, D])
    O['akp'] = dout('akp', [NA, PR, D])
    O['avp'] = dout('avp', [NA, PR, D])
    O['aikp'] = dout('aikp', [NA, PR, 64])
    O['bkp'] = dout('bkp', [max(NB, 1), PR, D])
    O['bvp'] = dout('bvp', [max(NB, 1), PR, D])
    O['convp'] = dout('convp', [DEPTH, 2, DFF])
    O['aks'] = dout('aks', [NS, NA, cfg.TS, D])
    O['avs'] = dout('avs', [NS, NA, cfg.TS, D])
    O['aiks'] = dout('aiks', [NS, NA, cfg.TS, 64])
    O['bks'] = dout('bks', [NS, max(NB, 1), cfg.TS, D])
    O['bvs'] = dout('bvs', [NS, max(NB, 1), cfg.TS, D])
    O['convs'] = dout('convs', [NS, DEPTH, 2, DFF])

    for s in streams:
        n = s.name
        alias = cfg.ALIAS and s.kind == 'p'
        s.X = dscr('X' + n, [16 if alias else s.NQ, D], F32)
        s.QT = dscr('QT' + n, [H, 128, s.NQ], BF16)
        s.KT = dscr('KT' + n, [H, 128, s.NK], BF16)
        s.V = dscr('V' + n, [H, 128, len(s.kblocks), 128], BF16)
        s.QIT = dscr('QIT' + n, [8 * 64, s.NQ], BF16)
        s.KIT = dscr('KIT' + n, [64, s.NK], BF16)
        s.WI = dscr('WI' + n, [s.NQ, 8], F32)
        s.OA = dscr('OA' + n, [16 if alias else s.NQ, D], BF16)
        if alias:
            oay = O['yp'].bitcast(BF16).rearrange("a (two d) -> (a two) d", two=2)
            assert tuple(oay.shape) == (cfg.S, D), oay.shape
            s.xrows = (lambda q0, n_, s=s: s.X[0:n_, :] if q0 == 0 else I['xp'][q0 - 16:q0 - 16 + n_, :])
            s.oarows = (lambda q0, n_, s=s, oay=oay: s.OA[0:n_, :] if q0 == 0 else oay[q0 - 16:q0 - 16 + n_, :])
        else:
            s.xrows = (lambda q0, n_, s=s: s.X[q0:q0 + n_, :])
            s.oarows = (lambda q0, n_, s=s: s.OA[q0:q0 + n_, :])
    VB = nc.dram_tensor('VB', [8, RR], F32)
    FO = dict(O)
    for s in streams:
        if s.kind == 'p':
            s.ko = {True: FO['akp'], False: FO['bkp']}
            s.vo = {True: FO['avp'], False: FO['bvp']}
            s.io = FO['aikp']
            s.convo = O['convp']
        else:
            i_ = s.sidx
            s.ko = {True: O['aks'][i_], False: O['bks'][i_]}
            s.vo = {True: O['avs'][i_], False: O['bvs'][i_]}
            s.io = O['aiks'][i_]
            s.convo = O['convs'][i_]
            s.ck = {True: I['cak'][i_], False: I['cbk'][i_]}
            s.cv = {True: I['cav'][i_], False: I['cbv'][i_]}
            s.cik = I['caik'][i_]
            s.sconv = I['sconv'][i_]
            s.yo = O['ys'][i_]

    def x0_src(s, q0, n):
        if s.kind == 's':
            return I['xs'][s.sidx, q0:q0 + n, :]
        if q0 == 0:
            return I['meta'][0:n, :]
        return I['xp'][q0 - 16:q0 - 16 + n, :]

    top = contextlib.ExitStack()
    nph = [0]

    def phase_done():
        nph[0] += 1
        return cfg.upto is not None and nph[0] >= cfg.upto

    with top:
        Phase.glob = Glob(nc, top)
        def ctile(shape, dt, name):
            return top.enter_context(nc.sbuf_tensor('c_' + name, list(shape), dt))

        ident_f = ctile([128, 128], F32, 'identf')
        ident_b = ctile([128, 128], BF16, 'identb')
        i30k = ctile([128, 128], BF16, 'i30k')
        negtri = ctile([128, 128], BF16, 'negtri')
        negones = ctile([128, 128], BF16, 'negones')
        vm = ctile([128, 128], F32, 'vm')
        vmpos = ctile([128, 128], F32, 'vmpos')
        pow2 = ctile([128, cfg.NIT + 1], F32, 'pow2')
        tz0 = ctile([128, 8, 128], BF16, 'tz0')
        tz1 = ctile([128, 8, 128], BF16, 'tz1')
        tzm = ctile([128, 8, 128], BF16, 'tzm')
        epsc = ctile([128, 1], F32, 'eps')

        ph = Phase(nc, 'init')
        jf = ph.sb([128, 128], F32, 'jf')
        ph.memset('pool', ident_f[:], 0.0, [ident_f])
        ph.op('pool', lambda e: e.affine_select(out=ident_f[:], in_=ident_f[:], pattern=[[-1, 128]], compare_op=ALU.not_equal, fill=1.0, base=0, channel_multiplier=1), [ident_f], [ident_f])
        ph.copy('dve', ident_b[:], ident_f[:], [ident_f], [ident_b])
        ph.ts('dve', i30k[:], ident_f[:], 30000.0, None, ALU.mult, None, [ident_f], [i30k])
        ph.memset('pool', jf[:], 0.0, [jf])
        ph.op('pool', lambda e: e.affine_select(out=jf[:], in_=jf[:], pattern=[[1, 128]], compare_op=ALU.not_equal, fill=1.0, base=-127, channel_multiplier=1), [jf], [jf])
        ph.memset('pool', negtri[:], -1.0, [negtri])
        ph.op('pool', lambda e: e.affine_select(out=negtri[:], in_=negtri[:], pattern=[[-1, 128]], compare_op=ALU.is_ge, fill=0.0, base=0, channel_multiplier=1), [negtri], [negtri])
        ph.memset('pool', negones[:], -1.0, [negones])
        ph.memset('dve', vm[:], 0.0, [vm])
        ph.memset('dve', vm[0:64, 64:128], -BIG, [vm])
        ph.memset('dve', vmpos[:], 0.0, [vmpos])
        ph.memset('dve', vmpos[0:64, 64:128], 2 * BIG, [vmpos])
        for k in range(cfg.NIT + 1):
            ph.memset('dve', pow2[:, k:k + 1], 2.0 ** (-k), [pow2])
        ph.memset('dve', epsc[:], 1e-5, [epsc])
        rb = ph.sb([32, 8], F32, 'rb')
        oh = ph.sb([32, RR], F32, 'oh')
        vsb = ph.sb([8, RR], F32, 'vsb')
        hk = ph.sb([128, 8, 128], F32, 'hk')
        pv = ph.ps([128, 512], F32, 'pv')
        pt2 = ph.ps([128, 2, 512], F32, 'pt2')
        ph.dma('sp', rb[:], I['relb'], [], [rb], 'rb')
        ph.dma('sp', oh[:], I['ohr'], [], [oh], 'oh')
        ph.mm(pv[0:8, 0:RR], rb[:], oh[:], True, True, [rb, oh], [pv])
        ph.ts('dve', vsb[:], pv[0:8, 0:RR], pv[0:8, RR - 1:RR], None, ALU.subtract, None, [pv], [vsb])
        ph.dma('sp', VB.ap(), vsb[:], [vsb], [VB], 'vb')
        for (tz, base) in ((tz0, 0), (tz1, 128), (tzm, 16)):
            src = bass.AP(tensor=VB, offset=base, ap=[[1, 128], [RR, 8], [1, 128]])
            ph.dma('sp', hk[:], src, [VB], [hk], 'hk')
            for hh in range(2):
                ph.mm(pt2[:, hh, :], jf[:], hk[:, 4 * hh:4 * hh + 4, :], True, True, [jf, hk], [(pt2, hh)])
            ph.copy('dve', tz[:], pt2[:], [(pt2, 0), (pt2, 1)], [tz])
        ph.finish()

        done = phase_done()
        li_a = 0
        li_b = 0
        for layer in range(DEPTH):
            if done:
                break
            isA = (layer % 2 == 0)
            j = layer // 2
            done = done or phase_cache(nc, cfg, I, streams[1:], isA, j, ident_f, phase_done)
            if done:
                break
            done = phase_p1(nc, cfg, I, O, streams, layer, isA, j, ident_f, x0_src, phase_done)
            if done:
                break
            if isA:
                done = phase_attn_a(nc, cfg, streams, ident_b, i30k, vm, vmpos, pow2, tz0, tz1, tzm, phase_done)
            else:
                done = phase_attn_b(nc, cfg, streams, negtri, negones, phase_done)
            if done:
                break
            done = phase_p2b(nc, cfg, I, streams, layer, isA, j, ident_b, epsc, x0_src, phase_done)
            if done:
                break
            done = phase_p3(nc, cfg, I, O, FO, streams, layer, ident_f, epsc, phase_done)
    return nc


def phase_cache(nc, cfg, I, sstreams, isA, j, ident_f, phase_done):
    ph = Phase(nc, 'cache')
    nb = cfg.PAST // 128
    kin = [ph.sb([128, 1024], F32, 'kin') for _ in range(2)]
    vin = [ph.sb([128, 1024], F32, 'vin') for _ in range(2)]
    kts = [ph.sb([128, 8, 128], BF16, 'kts') for _ in range(2)]
    vbs = [ph.sb([128, 1024], BF16, 'vbs') for _ in range(2)]
    pk = [ph.ps([128, 4, 128], F32, 'pk') for _ in range(4)]
    if isA:
        iin = [ph.sb([128, 64], F32, 'iin') for _ in range(2)]
        its = [ph.sb([64, 128], BF16, 'its') for _ in range(2)]
        pi = [ph.ps([128, 128], F32, 'pi') for _ in range(2)]
    bi = 0
    for ss in sstreams:
        ck = ss.ck[isA]
        cv = ss.cv[isA]
        for b in range(nb):
            s = bi % 2
            bi += 1
            ph.dma('sp', kin[s][:], ck[j, b * 128:(b + 1) * 128, :], [], [kin[s]], 'kin%d' % s)
            ph.dma('sp', vin[s][:], cv[j, b * 128:(b + 1) * 128, :], [], [vin[s]], 'vin%d' % s)
            if isA:
                ph.dma('sp', iin[s][:], ss.cik[j, b * 128:(b + 1) * 128, :], [], [iin[s]], 'iin%d' % s)
            for hh in range(2):
                p = pk[2 * s + hh]
                for h4 in range(4):
                    h = 4 * hh + h4
                    ph.tr(p[:, h4, :], kin[s][:, h * 128:(h + 1) * 128], ident_f[:], [kin[s], ident_f], [p])
                ph.evac(kts[s][:, 4 * hh:4 * hh + 4, :], p[:], [p], [kts[s]])
            ph.dma('pool', ss.KT[:, :, b * 128:(b + 1) * 128].rearrange("h d n -> d h n"), kts[s][:], [kts[s]], [], 'kts%d' % s)
            ph.copy('pool', vbs[s][:], vin[s][:], [vin[s]], [vbs[s]])
            ph.dma('pool', ss.V[:, :, b, :].rearrange("h l d -> l h d"), vbs[s][:].rearrange("l (h d) -> l h d", h=8), [vbs[s]], [], 'vbs%d' % s)
            if isA:
                ph.tr(pi[s][0:64, :], iin[s][:, :], ident_f[:], [iin[s], ident_f], [pi[s]])
                ph.evac(its[s][:], pi[s][0:64, :], [pi[s]], [its[s]])
                ph.dma('pool', ss.KIT[:, b * 128:(b + 1) * 128], its[s][:], [its[s]], [], 'its%d' % s)
    ph.finish()
    return phase_done()


CW = 704


def load_weight_bf16(ph, wsb, wdram, nk, ncols, stg, chanp):
    i = 0
    for c in range(nk):
        for c0 in range(0, ncols, CW):
            w = min(CW, ncols - c0)
            s = i % len(stg)
            ph.dma('sp', stg[s][:, 0:w], wdram[c * 128:(c + 1) * 128, c0:c0 + w], [], [stg[s]], '%s%d' % (chanp, s))
            eng = ('dve', 'pool', 'act')[i % 3]
            ph.copy(eng, wsb[:, c, c0:c0 + w], stg[s][:, 0:w], [stg[s]], [wsb])
            i += 1


def phase_p1(nc, cfg, I, O, streams, layer, isA, j, ident_f, x0_src, phase_done):
    ph = Phase(nc, 'p1')
    D = cfg.D
    NIN = cfg.AIN if isA else cfg.BIN
    W = (I['wain'] if isA else I['wbin'])[j]
    wsb = ph.sb([128, 8, NIN], BF16, 'w')
    stg = [ph.sb([128, CW], F32, 'wst') for _ in range(3)]
    load_weight_bf16(ph, wsb, W, 8, NIN, stg, 'wst')
    xin = [ph.sb([128, 4, 1024], F32, 'xin') for _ in range(2)]
    xT = ph.sb([128, 8, 512], BF16, 'xT')
    qts = ph.sb([128, 8, 512], BF16, 'qts')
    kts = ph.sb([128, 8, 512], BF16, 'kts')
    kout = [ph.sb([128, 1024], F32, 'kout') for _ in range(2)]
    vout = [ph.sb([128, 1024], F32, 'vout') for _ in range(2)]
    vbf = [ph.sb([128, 1024], BF16, 'vbf') for _ in range(2)]
    if isA:
        qis = ph.sb([128, 4, 512], BF16, 'qis')
        kis = ph.sb([64, 512], BF16, 'kis')
        kio = [ph.sb([128, 64], F32, 'kio') for _ in range(2)]
        wis = [ph.sb([128, 8], F32, 'wis') for _ in range(2)]
    pT = [ph.ps([128, 512], F32, 'pT') for _ in range(2)]
    pF = [ph.ps([128, 512], F32, 'pF') for _ in range(2)]
    pK = [ph.ps([128, 512], F32, 'pK') for _ in range(4)]
    ph.init_split(cfg, I, 2)

    work = []
    for si, s in enumerate(streams):
        for g in s.groups:
            work.append((si, s, g))

    def load(gi):
        si, s, g = work[gi]
        sl = gi % 2
        for t, qt in enumerate(g):
            q0, n = s.qtiles[qt]
            src = x0_src(s, q0, n) if layer == 0 else s.xrows(q0, n)
            ph.dma('sp', xin[sl][0:n, t, :], src, [], [xin[sl]], 'xin%d' % sl)

    load(0)
    tcount = 0
    for gi, (si, s, g) in enumerate(work):
        sl = gi % 2
        if gi + 1 < len(work):
            load(gi + 1)
        cols = []
        co = 0
        for qt in g:
            q0, n = s.qtiles[qt]
            cols.append((co, n, q0))
            co += n
        N = co
        gq0 = s.qtiles[g[0]][0]
        for c in range(8):
            p = pT[c % 2]
            for t, (co_t, n, q0) in enumerate(cols):
                ph.tr(p[:, co_t:co_t + n], xin[sl][0:n, t, c * 128:(c + 1) * 128], ident_f[0:n, 0:n], [xin[sl], ident_f], [p])
            ph.evac(xT[:, c, 0:N], p[:, 0:N], [p], [xT])
        for m in range(8):
            p = pF[m % 2]
            for c in range(8):
                ph.mm(p[:, 0:N], wsb[:, c, m * 128:(m + 1) * 128], xT[:, c, 0:N], c == 0, c == 7, [wsb, xT], [p])
            ph.act(qts[:, m, 0:N], p[:, 0:N], AF.Copy, [p], [qts], scale=128.0 ** -0.5)
        ph.dma('pool', s.QT[:, :, gq0:gq0 + N].rearrange("h d n -> d h n"), qts[:, :, 0:N], [qts], [], 'qts')
        for m in range(8):
            p = pF[m % 2]
            for c in range(8):
                ph.mm(p[:, 0:N], wsb[:, c, 1024 + m * 128:1024 + (m + 1) * 128], xT[:, c, 0:N], c == 0, c == 7, [wsb, xT], [p])
            ph.evac(kts[:, m, 0:N], p[:, 0:N], [p], [kts])
        ph.dma('pool', s.KT[:, :, s.koff + gq0:s.koff + gq0 + N].rearrange("h d n -> d h n"), kts[:, :, 0:N], [kts], [], 'kts')
        if isA:
            for m in range(4):
                p = pF[m % 2]
                for c in range(8):
                    ph.mm(p[:, 0:N], wsb[:, c, 3072 + m * 128:3072 + (m + 1) * 128], xT[:, c, 0:N], c == 0, c == 7, [wsb, xT], [p])
                ph.evac(qis[:, m, 0:N], p[:, 0:N], [p], [qis])
            ph.dma('pool', s.QIT[:, gq0:gq0 + N].rearrange("(m p) n -> p m n", p=128), qis[:, :, 0:N], [qis], [], 'qis')
            p = pF[0]
            for c in range(8):
                ph.mm(p[0:64, 0:N], wsb[:, c, 3584:3648], xT[:, c, 0:N], c == 0, c == 7, [wsb, xT], [p])
            ph.evac(kis[:, 0:N], p[0:64, 0:N], [p], [kis])
            ph.dma('pool', s.KIT[:, s.koff + gq0:s.koff + gq0 + N], kis[:, 0:N], [kis], [], 'kis')
        for t, (co_t, n, q0) in enumerate(cols):
            ts_ = tcount % 2
            tcount += 1
            kb = s.own[g[t]]
            for half in range(2):
                p = pK[half]
                for c in range(8):
                    ph.mm(p[0:n, :], xT[:, c, co_t:co_t + n], wsb[:, c, 1024 + half * 512:1024 + (half + 1) * 512], c == 0, c == 7, [xT, wsb], [p])
                ph.evac(kout[ts_][0:n, half * 512:(half + 1) * 512], p[0:n, :], [p], [kout[ts_]])
            ph.store_rows(cfg, s, lambda r0, s=s, n=n: s.ko[isA][j, r0:r0 + n, :], q0, n, kout[ts_][0:n, :], [kout[ts_]], 'kout%d' % ts_, key='k')
            for half in range(2):
                p = pK[2 + half]
                for c in range(8):
                    ph.mm(p[0:n, :], xT[:, c, co_t:co_t + n], wsb[:, c, 2048 + half * 512:2048 + (half + 1) * 512], c == 0, c == 7, [xT, wsb], [p])
                ph.evac(vout[ts_][0:n, half * 512:(half + 1) * 512], p[0:n, :], [p], [vout[ts_]])
            ph.copy('pool', vbf[ts_][0:n, :], vout[ts_][0:n, :], [vout[ts_]], [vbf[ts_]])
            ph.dma('pool', s.V[:, 0:n, kb, :].rearrange("h l d -> l h d"), vbf[ts_][0:n, :].rearrange("l (h d) -> l h d", h=8), [vbf[ts_]], [], 'vbf%d' % ts_)
            ph.store_rows(cfg, s, lambda r0, s=s, n=n: s.vo[isA][j, r0:r0 + n, :], q0, n, vout[ts_][0:n, :], [vout[ts_]], 'vout%d' % ts_, key='v')
            if isA:
                p = pT[0]
                for c in range(8):
                    ph.mm(p[0:n, 0:72], xT[:, c, co_t:co_t + n], wsb[:, c, 3584:3656], c == 0, c == 7, [xT, wsb], [p])
                ph.copy('dve', kio[ts_][0:n, :], p[0:n, 0:64], [p], [kio[ts_]])
                ph.act(wis[ts_][0:n, :], p[0:n, 64:72], AF.Copy, [p], [wis[ts_]], scale=512.0 ** -0.5)
                ph.store_rows(cfg, s, lambda r0, s=s, n=n: s.io[j, r0:r0 + n, :], q0, n, kio[ts_][0:n, :], [kio[ts_]], 'kio%d' % ts_, key='i')
                ph.dma('pool', s.WI[q0:q0 + n, :], wis[ts_][0:n, :], [wis[ts_]], [], 'wis%d' % ts_)
    ph.finish()
    return phase_done()


def kv_chunks(s, nblocks, CHB=16):
    out = []
    b = 0
    while b < nblocks:
        e = min(nblocks, b + CHB)
        out.append((b, e))
        b = e
    return out


def load_kv(ph, s, h, b0, b1, ktb, vab, sl):
    p0 = s.kblocks[b0][0]
    p1 = s.kblocks[b1 - 1][0] + s.kblocks[b1 - 1][1]
    ph.dma('sp', ktb[sl][:, 0:p1 - p0], s.KT[h, :, p0:p1], [], [ktb[sl]], 'ktb%d' % sl)
    b = b0
    while b < b1:
        n = s.kblocks[b][1]
        e = b + 1
        while e < b1 and s.kblocks[e][1] == n:
            e += 1
        ph.dma('sp', vab[sl][0:n, b - b0:e - b0, 0:128], s.V[h, 0:n, b:e, :], [], [vab[sl]], 'vab%d' % sl)
        b = e


def phase_attn_a(nc, cfg, streams, ident_b, i30k, vm, vmpos, pow2, tz0, tz1, tzm, phase_done):
    ph = Phase(nc, 'aa')
    NIT = cfg.NIT
    NKmax = max(s.NK for s in streams)
    NKBmax = max(len(s.kblocks) for s in streams)
    kit = ph.sb([64, NKmax], BF16, 'kit')
    maskT = ph.sb([128, NKBmax, 512], BF16, 'maskT')
    score = ph.sb([128, NKmax], F32, 'score')
    selm = ph.sb([128, NKmax], BF16, 'selm')
    qit = [ph.sb([64, 8, 128], BF16, 'qit') for _ in range(2)]
    wi = [ph.sb([128, 8], F32, 'wi') for _ in range(2)]
    qT = ph.sb([128, 8, 512], BF16, 'qT')
    ktb = [ph.sb([128, 2048], BF16, 'ktb') for _ in range(2)]
    vab = [ph.sb([128, 16, 129], BF16, 'vab') for _ in range(2)]
    rsb = [ph.sb([128, 2, 512], F32, 'rsb') for _ in range(3)]
    PT = [ph.sb([128, 512], BF16, 'PT') for _ in range(3)]
    osb = ph.sb([128, 4, 1024], BF16, 'osb')
    sm = ph.sb([128, 16], F32, 'sm')
    htab = ph.sb([128, NIT + 1], F32, 'htab')
    dtmp = ph.sb([128, 128], F32, 'dtmp')
    rc = ph.sb([128, 4], F32, 'rc')
    pI = ph.ps([128, 2, 512], F32, 'pI')
    pL = [ph.ps([128, 512], F32, 'pL') for _ in range(2)]
    pO = [ph.ps([128, 512], F32, 'pO') for _ in range(4)]
    for v_ in vab:
        ph.memset('pool', v_[:, :, 128:129], 1.0, [v_])
    zb = ph.sb([128, 512], BF16, 'zb')
    ph.memset('pool', zb[:], 0.0, [zb])
    MID, CNT, TT, HI, LO, LO2, THR, H0 = [sm[:, i:i + 1] for i in range(8)]

    for s in streams:
        topk = float(s.topk)
        ph.dma('sp', kit[:, 0:s.NK], s.KIT[:, :], [], [kit], 'kit')
        qtc = 0
        for g in s.groups:
            jmax = max(s.own[qt] for qt in g)
            nblk = jmax + 1
            cols = []
            co = 0
            for qt in g:
                cols.append(co)
                co += s.qtiles[qt][1]
            N = co
            gq0 = s.qtiles[g[0]][0]
            ph.dma('sp', qT[:, :, 0:N], s.QT[:, :, gq0:gq0 + N].rearrange("h d n -> d h n"), [], [qT], 'qT')
            for t, qt in enumerate(g):
                q0, nq = s.qtiles[qt]
                own = s.own[qt]
                L = s.kblocks[own][0] + s.kblocks[own][1]
                sl = qtc % 2
                qtc += 1
                ph.dma('sp', qit[sl][:, :, 0:nq], s.QIT[:, q0:q0 + nq].rearrange("(h d) n -> d h n", d=64), [], [qit[sl]], 'qit%d' % sl)
                ph.dma('sp', wi[sl][0:nq, :], s.WI[q0:q0 + nq, :], [], [wi[sl]], 'wi%d' % sl)
                ci = 0
                for c0 in range(0, L, 512):
                    w = min(512, L - c0)
                    for hh in range(4):
                        rs = rsb[ci % 3]
                        ci += 1
                        for h2 in range(2):
                            h = 2 * hh + h2
                            ph.mm(pI[0:nq, h2, 0:w], qit[sl][:, h, 0:nq], kit[:, c0:c0 + w], True, True, [qit[sl], kit], [(pI, h2)])
                        ph.act(rs[0:nq, :, 0:w], pI[0:nq, :, 0:w], AF.Relu, [(pI, 0), (pI, 1)], [rs])
                        for h2 in range(2):
                            h = 2 * hh + h2
                            if h == 0:
                                ph.ts('dve', score[0:nq, c0:c0 + w], rs[0:nq, 0, 0:w], wi[sl][0:nq, 0:1], None, ALU.mult, None, [rs, wi[sl]], [score])
                            else:
                                ph.stt(score[0:nq, c0:c0 + w], rs[0:nq, h2, 0:w], wi[sl][0:nq, h:h + 1], score[0:nq, c0:c0 + w], ALU.mult, ALU.add, [rs, wi[sl], score], [score])
                diag = s.chunked and nq == 128
                if diag:
                    ph.tt('dve', score[:, L - 128:L], score[:, L - 128:L], vm[:], ALU.add, [score, vm], [score])
                if L <= s.topk:
                    ph.memset('dve', THR[0:nq], -BIG / 2, [sm])
                else:
                    ph.op('dve', lambda e, nq=nq, L=L: e.tensor_reduce(out=HI[0:nq], in_=score[0:nq, 0:L], axis=AX.X, op=ALU.max), [score], [sm])
                    if diag:
                        ph.op('dve', lambda e, L=L: e.tensor_reduce(out=LO[:], in_=score[:, 0:L - 128], axis=AX.X, op=ALU.min), [score, sm], [sm])
                        ph.tt('dve', dtmp[:], score[:, L - 128:L], vmpos[:], ALU.add, [score, vmpos], [dtmp])
                        ph.op('dve', lambda e: e.tensor_reduce(out=LO2[:], in_=dtmp[:], axis=AX.X, op=ALU.min), [dtmp, sm], [sm])
                        ph.tt('dve', LO[:], LO[:], LO2[:], ALU.min, [sm], [sm])
                    else:
                        ph.op('dve', lambda e, nq=nq, L=L: e.tensor_reduce(out=LO[0:nq], in_=score[0:nq, 0:L], axis=AX.X, op=ALU.min), [score, sm], [sm])
                    ph.stt(MID[0:nq], HI[0:nq], 1.0, LO[0:nq], ALU.mult, ALU.add, [sm], [sm])
                    ph.ts('dve', MID[0:nq], MID[0:nq], 0.5, None, ALU.mult, None, [sm], [sm])
                    ph.stt(H0[0:nq], LO[0:nq], -1.0, HI[0:nq], ALU.mult, ALU.add, [sm], [sm])
                    ph.ts('dve', H0[0:nq], H0[0:nq], 0.5, None, ALU.mult, None, [sm], [sm])
                    ph.ts('dve', htab[0:nq, :], pow2[0:nq, :], H0[0:nq], None, ALU.mult, None, [sm, pow2], [htab])
                    for k in range(NIT):
                        ph.ts('dve', selm[0:nq, 0:L], score[0:nq, 0:L], MID[0:nq], 0.0, ALU.is_ge, ALU.add, [score, sm], [selm, sm], accum=CNT[0:nq])
                        ph.ts('dve', TT[0:nq], CNT[0:nq], topk - 0.5, -0.5, ALU.is_ge, ALU.add, [sm], [sm])
                        ph.stt(MID[0:nq], TT[0:nq], htab[0:nq, k:k + 1], MID[0:nq], ALU.mult, ALU.add, [sm, htab], [sm])
                    ph.tt('dve', THR[0:nq], MID[0:nq], htab[0:nq, NIT:NIT + 1], ALU.subtract, [sm, htab], [sm])
                ph.ts('dve', selm[0:nq, 0:L], score[0:nq, 0:L], THR[0:nq], -1.0, ALU.is_ge, ALU.add, [score, sm], [selm])
                b = 0
                ti = 0
                while b <= own:
                    nk = s.kblocks[b][1]
                    e_ = b + 1
                    while e_ <= own and e_ - b < 4 and s.kblocks[e_][1] == nk:
                        e_ += 1
                    p = pL[ti % 2]
                    ti += 1
                    pb = p[:].bitcast(BF16)
                    for i in range(b, e_):
                        k0 = s.kblocks[i][0]
                        ph.tr(pb[0:nk, (i - b) * 128:(i - b) * 128 + nq], selm[0:nq, k0:k0 + nk], ident_b[0:nq, 0:nq], [selm, ident_b], [p])
                    src = pb[0:nk, 0:(e_ - b) * 128].rearrange("p (b q) -> p b q", q=128)[:, :, 0:nq]
                    ph.evac(maskT[0:nk, b:e_, cols[t]:cols[t] + nq], src, [p], [maskT])
                    b = e_
                if own < jmax:
                    ph.memset('pool', maskT[:, own + 1:jmax + 1, cols[t]:cols[t] + nq], -1.0, [maskT])
            chunks = kv_chunks(s, nblk)
            lc = 0
            pti = 0
            hc = [(h, b0, b1) for h in range(8) for (b0, b1) in chunks]
            load_kv(ph, s, hc[0][0], hc[0][1], hc[0][2], ktb, vab, lc % 2)
            for hci, (h, b0, b1) in enumerate(hc):
                if True:
                    sl = lc % 2
                    lc += 1
                    if hci + 1 < len(hc):
                        load_kv(ph, s, hc[hci + 1][0], hc[hci + 1][1], hc[hci + 1][2], ktb, vab, lc % 2)
                    pbase = s.kblocks[b0][0]
                    for kb in range(b0, b1):
                        k0, nk = s.kblocks[kb]
                        p = pL[pti % 2]
                        pt = PT[pti % 3]
                        pti += 1
                        extra = []
                        for t, qt in enumerate(g):
                            own = s.own[qt]
                            nq = s.qtiles[qt][1]
                            if kb > own:
                                continue
                            if s.chunked:
                                if kb == own:
                                    extra.append((t, nq, tz0))
                                elif kb == own - 1 and nk == 128:
                                    extra.append((t, nq, tz1))
                                elif kb == 0 and own == 1:
                                    extra.append((t, nq, tzm))
                            else:
                                if kb == own:
                                    extra.append((t, nq, tz0))
                                elif kb == own - 1:
                                    extra.append((t, nq, tz1))
                        ph.mm(p[0:nk, 0:N], ktb[sl][:, k0 - pbase:k0 - pbase + nk], qT[:, h, 0:N], True, False, [ktb[sl], qT], [p])
                        ph.mm(p[0:nk, 0:N], i30k[0:nk, 0:nk], maskT[0:nk, kb, 0:N], False, len(extra) == 0, [i30k, maskT], [p])
                        for ei, (t, nq, tz) in enumerate(extra):
                            ph.mm(p[0:nk, cols[t]:cols[t] + nq], ident_b[0:nk, 0:nk], tz[0:nk, h, 0:nq], False, ei == len(extra) - 1, [ident_b, tz], [p])
                        ph.act(pt[0:nk, 0:N], p[0:nk, 0:N], AF.Exp, [p], [pt])
                        for t, qt in enumerate(g):
                            own = s.own[qt]
                            nq = s.qtiles[qt][1]
                            if kb > own:
                                continue
                            po = pO[t]
                            ph.mm(po[0:nq, 0:129], pt[0:nk, cols[t]:cols[t] + nq], vab[sl][0:nk, kb - b0, :], kb == 0, kb == own, [pt, vab[sl]], [po])
                if hci + 1 < len(hc) and hc[hci + 1][0] == h:
                    continue
                for t, qt in enumerate(g):
                    nq = s.qtiles[qt][1]
                    po = pO[t]
                    ph.op('dve', lambda e, nq=nq, t=t, po=po: e.reciprocal(out=rc[0:nq, t:t + 1], in_=po[0:nq, 128:129]), [po], [(rc, t)])
                    ph.ts('dve', osb[0:nq, t, h * 128:(h + 1) * 128], po[0:nq, 0:128], rc[0:nq, t:t + 1], None, ALU.mult, None, [po, (rc, t)], [osb])
            for t, qt in enumerate(g):
                q0, nq = s.qtiles[qt]
                ph.dma('pool', s.oarows(q0, nq), osb[0:nq, t, :], [osb], [], 'osb')
    ph.finish()
    return phase_done()


def phase_attn_b(nc, cfg, streams, negtri, negones, phase_done):
    ph = Phase(nc, 'ab')
    cm = ph.sb([128, 4, 512], BF16, 'cm')
    ph.memset('pool', cm[:], 1.0, [cm])
    for r in range(4):
        if r > 0:
            ph.memset('pool', cm[:, r, 0:128 * r], 0.0, [cm])
        sl_ = cm[:, r, 128 * r:128 * (r + 1)]
        ph.op('pool', lambda e, sl_=sl_: e.affine_select(out=sl_, in_=sl_, pattern=[[1, 128]], compare_op=ALU.is_gt, fill=0.0, base=0, channel_multiplier=-1), [cm], [cm])
    qT = ph.sb([128, 8, 512], BF16, 'qT')
    ktb = [ph.sb([128, 2048], BF16, 'ktb') for _ in range(2)]
    vab = [ph.sb([128, 16, 129], BF16, 'vab') for _ in range(2)]
    esb = [ph.sb([128, 512], F32, 'esb') for _ in range(2)]
    spb = [ph.sb([128, 512], BF16, 'spb') for _ in range(2)]
    aT = [ph.sb([128, 512], BF16, 'aT') for _ in range(2)]
    sacc = ph.sb([128, 512], BF16, 'sacc')
    osb = ph.sb([128, 4, 1024], BF16, 'osb')
    pZ = [ph.ps([128, 512], F32, 'pZ') for _ in range(2)]
    pD = [ph.ps([128, 512], F32, 'pD') for _ in range(2)]
    pO = [ph.ps([128, 512], F32, 'pO') for _ in range(4)]
    zb = ph.sb([128, 512], BF16, 'zb')
    ph.memset('pool', zb[:], 0.0, [zb])
    onec = ph.sb([128, 1], F32, 'onec')
    ph.memset('pool', onec[:], 1.0, [onec])
    for s in streams:
        for g in s.groups:
            jmax = max(s.own[qt] for qt in g)
            jmin = min(s.own[qt] for qt in g)
            nblk = jmax + 1
            cols = []
            co = 0
            for qt in g:
                cols.append(co)
                co += s.qtiles[qt][1]
            N = co
            gq0 = s.qtiles[g[0]][0]
            ph.dma('sp', qT[:, :, 0:N], s.QT[:, :, gq0:gq0 + N].rearrange("h d n -> d h n"), [], [qT], 'qT')
            bmin = 0
            if cfg.BWIN is not None:
                bmin = max(0, jmin - cfg.BWIN)
            chunks = kv_chunks(s, nblk)
            chunks = [(max(b0, bmin), b1) for (b0, b1) in chunks if b1 > bmin]
            lc = 0
            it = 0
            for h in range(8):
                ph.memset('pool', sacc[:], 0.0, [sacc])
                first = True
                for (b0, b1) in reversed(chunks):
                    sl = lc % 2
                    lc += 1
                    load_kv(ph, s, h, b0, b1, ktb, vab, sl)
                    pbase = s.kblocks[b0][0]
                    for kb in range(b1 - 1, b0 - 1, -1):
                        k0, nk = s.kblocks[kb]
                        z = pZ[it % 2]
                        d = pD[it % 2]
                        es = esb[it % 2]
                        spt = spb[it % 2]
                        at = aT[it % 2]
                        it += 1
                        kt = ktb[sl][:, k0 - pbase:k0 - pbase + nk]
                        ph.mm(z[0:nk, 0:N], kt, qT[:, h, 0:N], True, True, [ktb[sl], qT], [z])
                        ph.act(es[0:nk, 0:N], z[0:nk, 0:N], AF.Exp, [z], [es])
                        ph.act(spt[0:nk, 0:N], es[0:nk, 0:N], AF.Ln, [es, onec], [spt], bias=onec[0:nk, 0:1])
                        ingroup = kb >= jmin
                        if ingroup:
                            cmt = cm[0:nk, kb - jmin, 0:N]
                            ph.tt('pool', spt[0:nk, 0:N], spt[0:nk, 0:N], cmt, ALU.mult, [spt, cm], [spt])
                        ph.mm(d[0:nk, 0:N], kt, qT[:, h, 0:N], True, False, [ktb[sl], qT], [d])
                        ph.mm(d[0:nk, 0:N], negtri[0:nk, 0:nk], spt[0:nk, 0:N], False, first, [negtri, spt], [d])
                        if not first:
                            ph.mm(d[0:nk, 0:N], negones[:, 0:nk], sacc[:, 0:N], False, True, [negones, sacc], [d])
                        ph.act(at[0:nk, 0:N], d[0:nk, 0:N], AF.Exp, [d], [at])
                        if ingroup:
                            ph.tt('pool', at[0:nk, 0:N], at[0:nk, 0:N], cmt, ALU.mult, [at, cm], [at])
                        ph.tt('dve', sacc[0:nk, 0:N], sacc[0:nk, 0:N], spt[0:nk, 0:N], ALU.add, [sacc, spt], [sacc])
                        first = False
                        for t, qt in enumerate(g):
                            own = s.own[qt]
                            nq = s.qtiles[qt][1]
                            if kb > own:
                                continue
                            ph.mm(pO[t][0:nq, 0:128], at[0:nk, cols[t]:cols[t] + nq], vab[sl][0:nk, kb - b0, 0:128], kb == own, kb == bmin, [at, vab[sl]], [pO[t]])
                for t, qt in enumerate(g):
                    nq = s.qtiles[qt][1]
                    ph.evac(osb[0:nq, t, h * 128:(h + 1) * 128], pO[t][0:nq, 0:128], [pO[t]], [osb])
            for t, qt in enumerate(g):
                q0, nq = s.qtiles[qt]
                ph.dma('pool', s.oarows(q0, nq), osb[0:nq, t, :], [osb], [], 'osb')
    ph.finish()
    return phase_done()


def layer_norm(ph, n, t_, out, gsb, bsb, stats, mv, epsc, tmp):
    for hf in range(2):
        ph.op('dve', lambda e, hf=hf: e.bn_stats(out=stats[0:n, hf * 6:(hf + 1) * 6], in_=t_[0:n, hf * 512:(hf + 1) * 512]), [t_], [(stats, hf)])
    ph.op('dve', lambda e: e.bn_aggr(out=mv[0:n, 0:2], in_=stats[0:n, 0:12]), [(stats, 0), (stats, 1)], [mv])
    ph.act(mv[0:n, 2:3], mv[0:n, 1:2], AF.Sqrt, [mv, epsc], [mv], bias=epsc[0:n, 0:1])
    ph.op('dve', lambda e: e.reciprocal(out=mv[0:n, 2:3], in_=mv[0:n, 2:3]), [mv], [mv])
    ph.stt(mv[0:n, 3:4], mv[0:n, 0:1], -1.0, mv[0:n, 2:3], ALU.mult, ALU.mult, [mv], [mv])
    ph.act(tmp[0:n, :], t_[0:n, :], AF.Identity, [t_, mv], [tmp], bias=mv[0:n, 3:4], scale=mv[0:n, 2:3])
    ph.tt('pool', tmp[0:n, :], tmp[0:n, :], gsb[0:n, :], ALU.mult, [tmp, gsb], [tmp])
    ph.tt('dve', out[0:n, :], tmp[0:n, :], bsb[0:n, :], ALU.add, [tmp, bsb], [out])


def phase_p2b(nc, cfg, I, streams, layer, isA, j, ident_b, epsc, x0_src, phase_done):
    ph = Phase(nc, 'p2b')
    W = (I['waout'] if isA else I['wbout'])[j]
    wsb = ph.sb([128, 8, 1024], BF16, 'w')
    stg = [ph.sb([128, CW], F32, 'wst') for _ in range(3)]
    load_weight_bf16(ph, wsb, W, 8, 1024, stg, 'wst')
    gsb = ph.sb([128, 1024], F32, 'g')
    bsb = ph.sb([128, 1024], F32, 'b')
    ph.dma('sp', gsb[:], I['ln1g'][layer:layer + 1, :].partition_broadcast(128), [], [gsb], 'g')
    ph.dma('sp', bsb[:], I['ln1b'][layer:layer + 1, :].partition_broadcast(128), [], [bsb], 'b')
    oin = [ph.sb([128, 1024], BF16, 'oin') for _ in range(2)]
    xin = [ph.sb([128, 1024], F32, 'xin') for _ in range(2)]
    oT = [ph.sb([128, 8, 128], BF16, 'oT') for _ in range(2)]
    tsb = [ph.sb([128, 1024], F32, 'tsb') for _ in range(2)]
    tmp = [ph.sb([128, 1024], F32, 'tmp') for _ in range(2)]
    xo = [ph.sb([128, 1024], F32, 'xo') for _ in range(2)]
    stats = [ph.sb([128, 12], F32, 'stats') for _ in range(2)]
    mv = [ph.sb([128, 4], F32, 'mv') for _ in range(2)]
    pT = [ph.ps([128, 512], F32, 'pT') for _ in range(2)]
    pY = [ph.ps([128, 2, 512], F32, 'pY') for _ in range(2)]
    work = []
    for s in streams:
        for (q0, n) in s.qtiles:
            work.append((s, q0, n))

    def load(i):
        s, q0, n = work[i]
        sl = i % 2
        ph.dma('sp', oin[sl][0:n, :], s.oarows(q0, n), [], [oin[sl]], 'oin%d' % sl)
        src = x0_src(s, q0, n) if layer == 0 else s.xrows(q0, n)
        ph.dma('sp', xin[sl][0:n, :], src, [], [xin[sl]], 'xin%d' % sl)

    load(0)
    for i, (s, q0, n) in enumerate(work):
        sl = i % 2
        if i + 1 < len(work):
            load(i + 1)
        pb = pT[sl][:].bitcast(BF16)
        for c in range(8):
            ph.tr(pb[:, c * 128:c * 128 + n], oin[sl][0:n, c * 128:(c + 1) * 128], ident_b[0:n, 0:n], [oin[sl], ident_b], [pT[sl]])
        ph.evac(oT[sl][:, :, 0:n], pb[:, 0:1024].rearrange("p (c q) -> p c q", q=128)[:, :, 0:n], [pT[sl]], [oT[sl]])
        for half in range(2):
            for c in range(8):
                ph.mm(pY[sl][0:n, half, :], oT[sl][:, c, 0:n], wsb[:, c, half * 512:(half + 1) * 512], c == 0, c == 7, [oT[sl], wsb], [(pY[sl], half)])
        ph.stt(tsb[sl][0:n, :], xin[sl][0:n, :], cfg.ALPHA, pY[sl][0:n, :, :].rearrange("p a b -> p (a b)"), ALU.mult, ALU.add, [xin[sl], (pY[sl], 0), (pY[sl], 1)], [tsb[sl]])
        layer_norm(ph, n, tsb[sl], xo[sl], gsb, bsb, stats[sl], mv[sl], epsc, tmp[sl])
        ph.dma('pool', s.xrows(q0, n), xo[sl][0:n, :], [xo[sl]], [], 'xo%d' % sl)
    ph.finish()
    return phase_done()


def phase_p3(nc, cfg, I, O, FO, streams, layer, ident_f, epsc, phase_done):
    ph = Phase(nc, 'p3')
    D, DFF, NF = cfg.D, cfg.DFF, cfg.NF
    last = (layer == cfg.DEPTH - 1)
    wg = ph.sb([128, 8, DFF], BF16, 'wg')
    wu = ph.sb([128, 8, DFF], BF16, 'wu')
    wd = ph.sb([128, NF, 1024], BF16, 'wd')
    stg = [ph.sb([128, CW], F32, 'wst') for _ in range(2)]
    load_weight_bf16(ph, wg, I['wg'][layer], 8, DFF, stg, 'wst')
    load_weight_bf16(ph, wu, I['wu'][layer], 8, DFF, stg, 'wst')
    load_weight_bf16(ph, wd, I['wd'][layer], NF, 1024, stg, 'wst')
    gsb = ph.sb([128, 1024], F32, 'g')
    bsb = ph.sb([128, 1024], F32, 'b')
    ph.dma('sp', gsb[:], I['ln2g'][layer:layer + 1, :].partition_broadcast(128), [], [gsb], 'g')
    ph.dma('sp', bsb[:], I['ln2b'][layer:layer + 1, :].partition_broadcast(128), [], [bsb], 'b')
    cw = ph.sb([128, 3, NF], F32, 'cw')
    cb = ph.sb([128, NF], F32, 'cb')
    for r_ in range(3):
        ph.dma('sp', cw[:, r_, :], I['cw'][layer, r_, :].rearrange("(c p) -> p c", p=128), [], [cw], 'cw', slow=True)
    ph.dma('sp', cb[:], I['cb'][layer, :].rearrange("(c p) -> p c", p=128), [], [cb], 'cb', slow=True)
    gprev = ph.sb([128, NF, 2], F32, 'gprev')
    xin = [ph.sb([128, 2, 1024], F32, 'xin') for _ in range(2)]
    xT = ph.sb([128, 8, 256], BF16, 'xT')
    hT = ph.sb([128, NF, 256], BF16, 'hT')
    gs = [ph.sb([128, 258], F32, 'gs') for _ in range(2)]
    ga = [ph.sb([128, 256], F32, 'ga') for _ in range(2)]
    tsb = [ph.sb([128, 1024], F32, 'tsb')] * 2
    tmp = tsb
    xo = [ph.sb([128, 1024], F32, 'xo')] * 2
    stats = [ph.sb([128, 12], F32, 'stats') for _ in range(2)]
    mv = [ph.sb([128, 4], F32, 'mv') for _ in range(2)]
    pT = [ph.ps([128, 512], F32, 'pT') for _ in range(2)]
    pG = [ph.ps([128, 512], F32, 'pG') for _ in range(2)]
    pU = [ph.ps([128, 512], F32, 'pU') for _ in range(2)]
    pY = ph.ps([128, 2, 512], F32, 'pY')
    if last:
        ph.init_split(cfg, I, 1)
    work = []
    for si, s in enumerate(streams):
        for gi, g in enumerate(s.groups2):
            work.append((si, s, gi, g))

    def load(i):
        si, s, gi, g = work[i]
        sl = i % 2
        for t, qt in enumerate(g):
            q0, n = s.qtiles[qt]
            ph.dma('sp', xin[sl][0:n, t, :], s.xrows(q0, n), [], [xin[sl]], 'xin%d' % sl)

    load(0)
    tcount = 0
    for i, (si, s, gi, g) in enumerate(work):
        sl = i % 2
        if i + 1 < len(work):
            load(i + 1)
        cols = []
        co = 0
        for qt in g:
            q0, n = s.qtiles[qt]
            cols.append((co, n, q0))
            co += n
        N = co
        lastg = (gi == len(s.groups2) - 1)
        if gi == 0:
            if s.kind == 'p':
                ph.memset('pool', gprev[:], 0.0, [gprev])
            else:
                for r_ in range(2):
                    ph.dma('sp', gprev[:, :, r_], s.sconv[layer, r_, :].rearrange("(c p) -> p c", p=128), [], [gprev], 'gprev', slow=True)
        for c in range(8):
            p = pT[c % 2]
            for t, (co_t, n, q0) in enumerate(cols):
                ph.tr(p[:, co_t:co_t + n], xin[sl][0:n, t, c * 128:(c + 1) * 128], ident_f[0:n, 0:n], [xin[sl], ident_f], [p])
            ph.evac(xT[:, c, 0:N], p[:, 0:N], [p], [xT])
        for f in range(NF):
            fs = f % 2
            pg = pG[fs]
            pu = pU[fs]
            for c in range(8):
                ph.mm(pg[:, 0:N], wg[:, c, f * 128:(f + 1) * 128], xT[:, c, 0:N], c == 0, c == 7, [wg, xT], [pg])
            for c in range(8):
                ph.mm(pu[:, 0:N], wu[:, c, f * 128:(f + 1) * 128], xT[:, c, 0:N], c == 0, c == 7, [wu, xT], [pu])
            g_ = gs[fs]
            ph.copy('pool', g_[:, 0:2], gprev[:, f, :], [gprev], [g_])
            ph.act(g_[:, 2:2 + N], pg[:, 0:N], AF.Copy, [pg], [g_])
            ph.copy('pool', gprev[:, f, :], g_[:, N:N + 2], [g_], [gprev])
            a_ = ga[fs]
            ph.ts('dve', a_[:, 0:N], g_[:, 2:2 + N], cw[:, 2, f:f + 1], cb[:, f:f + 1], ALU.mult, ALU.add, [g_, cw, cb], [a_])
            ph.stt(a_[:, 0:N], g_[:, 1:1 + N], cw[:, 1, f:f + 1], a_[:, 0:N], ALU.mult, ALU.add, [g_, cw, a_], [a_])
            ph.stt(a_[:, 0:N], g_[:, 0:N], cw[:, 0, f:f + 1], a_[:, 0:N], ALU.mult, ALU.add, [g_, cw, a_], [a_])
            ph.act(a_[:, 0:N], a_[:, 0:N], AF.Gelu_apprx_tanh, [a_], [a_])
            ph.tt('dve', hT[:, f, 0:N], a_[:, 0:N], pu[:, 0:N], ALU.mult, [a_, pu], [hT])
        if lastg:
            for r_ in range(2):
                dst = s.convo[layer, r_, :].rearrange("(c p) -> p c", p=128)
                ph.dma('pool', dst, gprev[:, :, r_], [gprev], [], 'gpo', slow=True)
        for t, (co_t, n, q0) in enumerate(cols):
            ts_ = tcount % 2
            tcount += 1
            for half in range(2):
                for f in range(NF):
                    ph.mm(pY[0:n, half, :], hT[:, f, co_t:co_t + n], wd[:, f, half * 512:(half + 1) * 512], f == 0, f == NF - 1, [hT, wd], [(pY, half)])
            ph.stt(tsb[ts_][0:n, :], xin[sl][0:n, t, :], cfg.ALPHA, pY[0:n, :, :].rearrange("p a b -> p (a b)"), ALU.mult, ALU.add, [xin[sl], (pY, 0), (pY, 1)], [tsb[ts_]])
            layer_norm(ph, n, tsb[ts_], xo[ts_], gsb, bsb, stats[ts_], mv[ts_], epsc, tmp[ts_])
            if not last:
                ph.dma('pool', s.xrows(q0, n), xo[ts_][0:n, :], [xo[ts_]], [], 'xo%d' % ts_)
            else:
                if s.kind == 's':
                    ph.dma('pool', s.yo[q0:q0 + n, :], xo[ts_][0:n, :], [xo[ts_]], [], 'xo%d' % ts_)
                elif q0 >= 16:
                    ph.store_rows(cfg, s, lambda r0, n=n: FO['yp'][r0:r0 + n, :], q0, n, xo[ts_][0:n, :], [xo[ts_]], 'xo%d' % ts_, yrow=True, key='y')
    ph.finish()
    return phase_done()


def phase_combine(nc, cfg, I, O, FO):
    ph = Phase(nc, 'comb')
    mk = ph.sb([128, 2], F32, 'mk')
    ph.dma('sp', mk[:], I['maskf'].partition_broadcast(128), [], [mk], 'mk')
    ta = [ph.sb([128, 1024], F32, 'ta') for _ in range(2)]
    tb = [ph.sb([128, 1024], F32, 'tb') for _ in range(2)]
    to = [ph.sb([128, 1024], F32, 'to') for _ in range(2)]
    items = []
    for j in range(cfg.NA):
        items += [(FO['akp'][j], O['akp'][j], cfg.PH, 1024), (FO['avp'][j], O['avp'][j], cfg.PH, 1024), (FO['aikp'][j], O['aikp'][j], cfg.PH, 64)]
    for j in range(cfg.NB):
        items += [(FO['bkp'][j], O['bkp'][j], cfg.PH, 1024), (FO['bvp'][j], O['bvp'][j], cfg.PH, 1024)]
    items.append((FO['yp'], O['yp'], cfg.PH - 16, 1024))
    work = []
    for (full, out, split, w) in items:
        rfull = full.shape[0]
        nA = split
        nB = rfull - split
        for r0 in range(0, nA, 128):
            n = min(128, nA - r0)
            nb = min(n, max(0, nB - r0))
            work.append((full, out, split, w, r0, n, nb))

    def load(i):
        full, out, split, w, r0, n, nb = work[i]
        sl = i % 2
        ph.dma('sp', ta[sl][0:n, 0:w], full[r0:r0 + n, :], [], [ta[sl]], 'ta%d' % sl)
        if nb > 0:
            ph.dma('sp', tb[sl][0:nb, 0:w], full[split + r0:split + r0 + nb, :], [], [tb[sl]], 'tb%d' % sl)

    load(0)
    for i, (full, out, split, w, r0, n, nb) in enumerate(work):
        sl = i % 2
        if i + 1 < len(work):
            load(i + 1)
        ph.act(to[sl][0:n, 0:w], ta[sl][0:n, 0:w], AF.Copy, [ta[sl], mk], [to[sl]], scale=mk[0:n, 0:1])
        if nb > 0:
            ph.stt(to[sl][0:nb, 0:w], tb[sl][0:nb, 0:w], mk[0:nb, 1:2], to[sl][0:nb, 0:w], ALU.mult, ALU.add, [tb[sl], mk, to[sl]], [to[sl]])
        ph.dma('pool', out[r0:r0 + n, :], to[sl][0:n, 0:w], [to[sl]], [], 'to%d' % sl)
    ph.finish()


_cache = {}


def make_in_maps(cfg, inputs, ncores):
    f = lambda a: np.ascontiguousarray(np.asarray(a, dtype=np.float32))
    B = inputs['x_prompt'].shape[0]
    rel = 127 - np.arange(RR)
    bk = rel_bucket_np(rel)
    ohr = np.zeros((32, RR), np.float32)
    ohr[bk, np.arange(RR)] = 1.0
    NA, NB, NS = cfg.NA, cfg.NB, cfg.NSAMP
    NBm = max(NB, 1)
    shared = {
        'meta': f(inputs['meta_tokens']),
        'relb': f(inputs['rel_bias']),
        'wain': f(inputs['w_a_in']),
        'waout': f(inputs['w_a_out']),
        'wbin': f(inputs['w_b_in']),
        'wbout': f(inputs['w_b_out']),
        'ln1g': f(inputs['ln1_g']), 'ln1b': f(inputs['ln1_b']),
        'ln2g': f(inputs['ln2_g']), 'ln2b': f(inputs['ln2_b']),
        'wg': f(inputs['w_ffn_gate']), 'wu': f(inputs['w_ffn_up']),
        'cw': f(inputs['ffn_conv_w']), 'cb': f(inputs['ffn_conv_b']),
        'wd': f(inputs['w_ffn_down']),
        'ohr': ohr,
    }
    maps = []
    for c in range(ncores):
        bp = c % B
        sl = slice(c * NS, (c + 1) * NS)
        m = dict(shared)
        if cfg.SPLIT:
            half = c // B
            m['maskf'] = np.array([[1 - half, half]], dtype=np.float32)
        m['xp'] = np.concatenate([f(inputs['x_prompt'][bp]).reshape(cfg.S // 2, 2 * cfg.D), np.zeros((1, 2 * cfg.D), np.float32)], 0)
        m['xs'] = f(inputs['x_sample'][sl])
        m['cak'] = f(np.asarray(inputs['cache_a_k'])[:, sl]).transpose(1, 0, 2, 3, 4).reshape(NS, NA, cfg.PAST, cfg.D)
        m['cav'] = f(np.asarray(inputs['cache_a_v'])[:, sl]).transpose(1, 0, 2, 3, 4).reshape(NS, NA, cfg.PAST, cfg.D)
        m['caik'] = np.ascontiguousarray(f(np.asarray(inputs['cache_a_idx_k'])[:, sl]).transpose(1, 0, 2, 3))
        m['cbk'] = f(np.asarray(inputs['cache_b_k'])[:, sl]).transpose(1, 0, 2, 3, 4).reshape(NS, NBm, cfg.PAST, cfg.D)
        m['cbv'] = f(np.asarray(inputs['cache_b_v'])[:, sl]).transpose(1, 0, 2, 3, 4).reshape(NS, NBm, cfg.PAST, cfg.D)
        m['sconv'] = np.ascontiguousarray(f(np.asarray(inputs['state_ffn_conv'])[:, sl]).transpose(1, 0, 2, 3))
        for k in ('cak', 'cav', 'cbk', 'cbv'):
            m[k] = np.ascontiguousarray(m[k])
        maps.append(m)
    return maps


def assemble(cfg, res, B, ncores):
    NA, NB, P, TS, D, NS = cfg.NA, cfg.NB, cfg.P, cfg.TS, cfg.D, cfg.NSAMP
    r = res
    pc = list(range(B))
    g = lambda key, c: np.asarray(r[c][key], dtype=np.float32)
    if cfg.SPLIT:
        def stp(key):
            outs = []
            for b in pc:
                a0, a1 = g(key, b), g(key, b + B)
                if key == 'yp':
                    outs.append(np.concatenate([a0[:cfg.PH - 16], a1[:cfg.P - cfg.PH]], 0))
                elif key == 'convp':
                    outs.append(a0)
                else:
                    outs.append(np.concatenate([a0[:, :cfg.PH], a1[:, :cfg.P - cfg.PH]], 1))
            return np.stack(outs, 0)
    else:
        stp = lambda key: np.stack([g(key, c) for c in pc], 0)
    sts = lambda key: np.concatenate([g(key, c) for c in range(ncores)], 0)
    y_p = stp('yp')
    y_s = sts('ys')
    akp = stp('akp').transpose(1, 0, 2, 3).reshape(NA, B, P, 8, 128)
    avp = stp('avp').transpose(1, 0, 2, 3).reshape(NA, B, P, 8, 128)
    aikp = stp('aikp').transpose(1, 0, 2, 3)
    bkp = stp('bkp').transpose(1, 0, 2, 3)[:NB].reshape(NB, B, P, 8, 128)
    bvp = stp('bvp').transpose(1, 0, 2, 3)[:NB].reshape(NB, B, P, 8, 128)
    convp = stp('convp').transpose(1, 0, 2, 3)
    nsq = ncores * NS
    aks = sts('aks').transpose(1, 0, 2, 3).reshape(NA, nsq, TS, 8, 128)
    avs = sts('avs').transpose(1, 0, 2, 3).reshape(NA, nsq, TS, 8, 128)
    aiks = sts('aiks').transpose(1, 0, 2, 3)
    bks = sts('bks').transpose(1, 0, 2, 3)[:NB].reshape(NB, nsq, TS, 8, 128)
    bvs = sts('bvs').transpose(1, 0, 2, 3)[:NB].reshape(NB, nsq, TS, 8, 128)
    convs = sts('convs').transpose(1, 0, 2, 3)
    return tuple(np.ascontiguousarray(a) for a in (y_p, y_s, akp, avp, aikp, bkp, bvp, convp, aks, avs, aiks, bks, bvs, convs))


def kernel(**inputs):
    cfg = Cfg()
    cfg.NSAMP = 1
    cfg.SPLIT = True
    cfg.ALIAS = True
    ncores = 8
    nc = build(cfg)
    maps = make_in_maps(cfg, inputs, ncores)
    res = run_bass_kernel_spmd(nc, maps, core_ids=list(range(ncores)))
    return assemble(cfg, res.results, inputs['x_prompt'].shape[0], ncores)
```
